# Optimizing a Trainium2 kernel written in Bass

```python
import jax, jax.numpy as jnp
from jax import lax
import numpy as np

D_MODEL = 1024
BATCH = 32
SEQ = 2048
DEPTH = 2

HEAD_DIM = D_MODEL // 16
ATTN_HEADS = 6
ATTN_WIDTH = ATTN_HEADS * HEAD_DIM
ATTN_PATTERNS = ((128, 1), (512, 4), (2048, 16))
HEADS_PER_PATTERN = ATTN_HEADS // len(ATTN_PATTERNS)
CONV_GROUPS = 4
CONV_WIDTH = CONV_GROUPS * HEAD_DIM
CONV_TAPS = 31
LRU_HEADS = 6
LRU_WIDTH = LRU_HEADS * HEAD_DIM
LRU_CONV_TAPS = 4
LRU_C = 8.0
MIX_WIDTH = ATTN_WIDTH + CONV_WIDTH + LRU_WIDTH
D_FF = 4 * D_MODEL
RMS_EPS = 1e-6
LN_EPS = 1e-5

Q0 = 0
K0 = Q0 + ATTN_WIDTH
V0 = K0 + ATTN_WIDTH
CA0 = V0 + ATTN_WIDTH
CG0 = CA0 + CONV_WIDTH
LG0 = CG0 + CONV_WIDTH
LX0 = LG0 + LRU_WIDTH
IN_COLS = LX0 + LRU_WIDTH

kernel_name = "hymba_style_conv_dilattn_rglru_block"


def rmsnorm(x, g):
    xf = x.astype(jnp.float32)
    y = xf * lax.rsqrt(jnp.mean(xf * xf, axis=-1, keepdims=True) + RMS_EPS)
    return y.astype(x.dtype) * g


def layernorm(x, g, b):
    xf = x.astype(jnp.float32)
    mu = jnp.mean(xf, axis=-1, keepdims=True)
    var = jnp.mean(jnp.square(xf - mu), axis=-1, keepdims=True)
    y = (xf - mu) * lax.rsqrt(var + LN_EPS)
    return y.astype(x.dtype) * g + b


def alibi_slopes(n):
    return jnp.asarray(2.0 ** (-8.0 * np.arange(1, n + 1) / n), dtype=jnp.float32)


def causal_depthwise_conv(x, w, b):
    K, C = w.shape
    y = lax.conv_general_dilated(x, w[:, None, :], window_strides=(1,), padding=[(K - 1, 0)],
                                 dimension_numbers=("NWC", "WIO", "NWC"), feature_group_count=C)
    return y + b


def dilated_window_attention(q, k, v, slopes, window, dilation):
    B, S, H, Dh = q.shape
    W = window // dilation
    L = S // dilation
    nb = -(-L // W)
    Lp = nb * W

    def to_blocks(t):
        t = t.reshape(B, L, dilation, H, Dh).transpose(0, 2, 3, 1, 4)
        t = jnp.pad(t, ((0, 0), (0, 0), (0, 0), (0, Lp - L), (0, 0)))
        return t.reshape(B, dilation, H, nb, W, Dh)

    def with_prev(t):
        prev = jnp.pad(t, ((0, 0), (0, 0), (0, 0), (1, 0), (0, 0), (0, 0)))[:, :, :, :-1]
        return jnp.concatenate([prev, t], axis=4)

    qb = to_blocks(q)
    kb = with_prev(to_blocks(k))
    vb = with_prev(to_blocks(v))
    s = jnp.einsum("brhnqc,brhnkc->brhnqk", qb, kb).astype(jnp.float32) * (Dh ** -0.5)
    qi = jnp.arange(W)[:, None]
    kj = jnp.arange(2 * W)[None, :]
    dist = qi + W - kj
    band = (dist >= 0) & (dist <= W)
    has_prev = (jnp.arange(nb)[:, None, None] > 0) | (kj[None] >= W)
    valid = band[None] & has_prev
    bias = -slopes[:, None, None] * (dilation * dist).astype(jnp.float32)[None]
    s = jnp.where(valid[None, None, None], s + bias[None, None, :, None], -jnp.inf)
    m = jnp.max(s, axis=-1, keepdims=True)
    p = jnp.exp(s - m)
    l = jnp.sum(p, axis=-1, keepdims=True)
    o = jnp.einsum("brhnqk,brhnkc->brhnqc", p, vb.astype(jnp.float32)) / l
    lse = (m + jnp.log(l))[..., 0]
    o = o.reshape(B, dilation, H, Lp, Dh)[:, :, :, :L].transpose(0, 3, 1, 2, 4).reshape(B, S, H, Dh)
    lse = lse.reshape(B, dilation, H, Lp)[..., :L].transpose(0, 3, 1, 2).reshape(B, S, H)
    return o, lse


def attention_mixer(q, k, v):
    B, S = q.shape[:2]
    slopes = alibi_slopes(ATTN_HEADS)
    outs, lses = [], []
    for g, (window, dilation) in enumerate(ATTN_PATTERNS):
        sl = slice(g * HEADS_PER_PATTERN, (g + 1) * HEADS_PER_PATTERN)
        o, lse = dilated_window_attention(q[:, :, sl], k[:, :, sl], v[:, :, sl], slopes[sl], window, dilation)
        outs.append(o)
        lses.append(lse)
    o = jnp.stack(outs, axis=2)
    lse = jnp.stack(lses, axis=2)
    alpha = jax.nn.softmax(lse, axis=2)
    return (o * alpha[..., None]).reshape(B, S, ATTN_WIDTH).astype(q.dtype)


def conv_module(a, gate, dw_w, dw_b, ln_g, ln_b):
    u = a * jax.nn.sigmoid(gate)
    u = causal_depthwise_conv(u, dw_w, dw_b)
    u = layernorm(u, ln_g, ln_b)
    return jax.nn.silu(u)


def rg_lru(x, w_a, b_a, w_x, b_x, lam):
    B, S, C = x.shape
    xh = x.reshape(B, S, LRU_HEADS, C // LRU_HEADS)
    r = jax.nn.sigmoid(jnp.einsum("bshi,hio->bsho", xh, w_a).reshape(B, S, C) + b_a)
    i = jax.nn.sigmoid(jnp.einsum("bshi,hio->bsho", xh, w_x).reshape(B, S, C) + b_x)
    log_a = -LRU_C * r.astype(jnp.float32) * jax.nn.softplus(-lam.astype(jnp.float32))
    a = jnp.exp(log_a)
    b = jnp.sqrt(-jnp.expm1(2.0 * log_a)) * (i * x).astype(jnp.float32)

    def combine(left, right):
        a1, b1 = left
        a2, b2 = right
        return a1 * a2, a2 * b1 + b2

    _, h = lax.associative_scan(combine, (a, b), axis=1)
    return h.astype(x.dtype)


def recurrent_mixer(gate, xr, cw, cb, w_a, b_a, w_x, b_x, lam):
    u = causal_depthwise_conv(xr, cw, cb)
    return jax.nn.gelu(gate) * rg_lru(u, w_a, b_a, w_x, b_x, lam)


def hybrid_layer(x, norm1_g, w_in, conv_dw_w, conv_dw_b, conv_ln_g, conv_ln_b,
                 lru_conv_w, lru_conv_b, lru_wa, lru_ba, lru_wx, lru_bx, lru_lambda,
                 w_out, norm2_g, w_up, w_down):
    B, S, _ = x.shape
    h = rmsnorm(x, norm1_g)
    z = jnp.einsum("bsd,dc->bsc", h, w_in)
    q = z[..., Q0:K0].reshape(B, S, ATTN_HEADS, HEAD_DIM)
    k = z[..., K0:V0].reshape(B, S, ATTN_HEADS, HEAD_DIM)
    v = z[..., V0:CA0].reshape(B, S, ATTN_HEADS, HEAD_DIM)
    y_attn = attention_mixer(q, k, v)
    y_conv = conv_module(z[..., CA0:CG0], z[..., CG0:LG0], conv_dw_w, conv_dw_b, conv_ln_g, conv_ln_b)
    y_lru = recurrent_mixer(z[..., LG0:LX0], z[..., LX0:IN_COLS], lru_conv_w, lru_conv_b,
                            lru_wa, lru_ba, lru_wx, lru_bx, lru_lambda)
    mix = jnp.concatenate([y_attn, y_conv, y_lru], axis=-1)
    x = x + jnp.einsum("bsc,cd->bsd", mix, w_out)
    h2 = rmsnorm(x, norm2_g)
    ff = jnp.square(jax.nn.relu(jnp.einsum("bsd,df->bsf", h2, w_up)))
    return x + jnp.einsum("bsf,fd->bsd", ff, w_down)


def setup_inputs(seed: int = 0) -> dict:
    key = jax.random.key(seed)
    ks = jax.random.split(key, 20)
    n = jax.random.normal
    f32 = jnp.float32
    x = n(ks[0], (BATCH, SEQ, D_MODEL), f32)
    norm1_g = 1.0 + 0.02 * n(ks[1], (DEPTH, D_MODEL), f32)
    w_in = n(ks[2], (DEPTH, D_MODEL, IN_COLS), f32) * D_MODEL ** -0.5
    conv_dw_w = n(ks[3], (DEPTH, CONV_TAPS, CONV_WIDTH), f32) * CONV_TAPS ** -0.5
    conv_dw_b = 0.02 * n(ks[4], (DEPTH, CONV_WIDTH), f32)
    conv_ln_g = 1.0 + 0.02 * n(ks[5], (DEPTH, CONV_WIDTH), f32)
    conv_ln_b = 0.02 * n(ks[6], (DEPTH, CONV_WIDTH), f32)
    lru_conv_w = n(ks[7], (DEPTH, LRU_CONV_TAPS, LRU_WIDTH), f32) * LRU_CONV_TAPS ** -0.5
    lru_conv_b = 0.02 * n(ks[8], (DEPTH, LRU_WIDTH), f32)
    blk = LRU_WIDTH // LRU_HEADS
    lru_wa = n(ks[9], (DEPTH, LRU_HEADS, blk, blk), f32) * blk ** -0.5
    lru_ba = 0.02 * n(ks[10], (DEPTH, LRU_WIDTH), f32)
    lru_wx = n(ks[11], (DEPTH, LRU_HEADS, blk, blk), f32) * blk ** -0.5
    lru_bx = 0.02 * n(ks[12], (DEPTH, LRU_WIDTH), f32)
    a_c = jax.random.uniform(ks[13], (DEPTH, LRU_WIDTH), f32, 0.9, 0.999)
    a0 = a_c ** (1.0 / LRU_C)
    lru_lambda = jnp.log(a0) - jnp.log1p(-a0)
    w_out = n(ks[14], (DEPTH, MIX_WIDTH, D_MODEL), f32) * MIX_WIDTH ** -0.5
    norm2_g = 1.0 + 0.02 * n(ks[15], (DEPTH, D_MODEL), f32)
    w_up = n(ks[16], (DEPTH, D_MODEL, D_FF), f32) * D_MODEL ** -0.5
    w_down = n(ks[17], (DEPTH, D_FF, D_MODEL), f32) * D_FF ** -0.5
    final_g = 1.0 + 0.02 * n(ks[18], (D_MODEL,), f32)
    return {"x": x, "norm1_g": norm1_g, "w_in": w_in, "conv_dw_w": conv_dw_w, "conv_dw_b": conv_dw_b,
            "conv_ln_g": conv_ln_g, "conv_ln_b": conv_ln_b, "lru_conv_w": lru_conv_w, "lru_conv_b": lru_conv_b,
            "lru_wa": lru_wa, "lru_ba": lru_ba, "lru_wx": lru_wx, "lru_bx": lru_bx, "lru_lambda": lru_lambda,
            "w_out": w_out, "norm2_g": norm2_g, "w_up": w_up, "w_down": w_down, "final_g": final_g}


def reference(x, norm1_g, w_in, conv_dw_w, conv_dw_b, conv_ln_g, conv_ln_b, lru_conv_w, lru_conv_b,
              lru_wa, lru_ba, lru_wx, lru_bx, lru_lambda, w_out, norm2_g, w_up, w_down, final_g):
    for l in range(DEPTH):
        x = hybrid_layer(x, norm1_g[l], w_in[l], conv_dw_w[l], conv_dw_b[l], conv_ln_g[l], conv_ln_b[l],
                         lru_conv_w[l], lru_conv_b[l], lru_wa[l], lru_ba[l], lru_wx[l], lru_bx[l], lru_lambda[l],
                         w_out[l], norm2_g[l], w_up[l], w_down[l])
    return rmsnorm(x, final_g)
```

```python
import numpy as np
from contextlib import ExitStack
import concourse.bass as bass
import concourse.mybir as mybir
from concourse.bass_utils import run_bass_kernel_spmd

F32 = mybir.dt.float32
BF16 = mybir.dt.bfloat16
AF = mybir.ActivationFunctionType
ALU = mybir.AluOpType
AX = mybir.AxisListType

D = 1024
S = 2048
DEPTH = 2
NCORES = 8
BATCH = 32
SEQ_PER_CORE = BATCH // NCORES
DFF = 4096
IN_COLS = 2432
DIL = (1, 4, 16)
NBLK = 91
BLK_OUT = 19
BLK_UP = 27
BLK_DOWN = 59
NSP = 108
SP_G1, SP_G2, SP_CW, SP_CB, SP_LNG, SP_LNB, SP_LCW, SP_LCB, SP_LBA, SP_LBX, SP_LAM = 0, 8, 16, 78, 80, 82, 84, 96, 99, 102, 105
RMS_EPS = 1e-6
LN_EPS = 1e-5
MASK_NEG = -30000.0


class Dep:
    __slots__ = ("w", "r", "excl")

    def __init__(self, excl=False):
        self.w = None
        self.r = {}
        self.excl = excl


class Eng:
    def __init__(self, name, be, semidx):
        self.name = name
        self.be = be
        self.semidx = semidx
        self.tick = 0
        self.waited = {}
        self.pending = False


class FW:
    def __init__(self, nc, es):
        self.nc = nc
        self.es = es
        self.sems = []
        self.semcum = []
        self.eng = {}
        for name, be in (("pe", nc.tensor), ("act", nc.scalar), ("dve", nc.vector),
                         ("pool", nc.gpsimd), ("sp", nc.sync)):
            self.eng[name] = Eng(name, be, self.new_sem("e_" + name))
        self.nwaits = 0
        self.nops = 0

    def new_sem(self, name):
        h = self.es.enter_context(self.nc.semaphore(name))
        self.sems.append(h)
        self.semcum.append(0)
        return len(self.sems) - 1

    def _wait(self, e, need):
        for s, v in need.items():
            if e.waited.get(s, 0) < v:
                e.be.wait_ge(self.sems[s], v)
                e.waited[s] = v
                self.nwaits += 1

    enabled = True

    def op(self, en, fn, reads=(), writes=(), signal=True):
        if not self.enabled:
            return None
        if any(d.excl for d in reads):
            writes = list(writes) + [d for d in reads if d.excl and d not in writes]
        e = self.eng[en]
        me = e.semidx
        need = {}
        for d in reads:
            if d.w is not None:
                s, v = d.w
                if need.get(s, 0) < v:
                    need[s] = v
        skip = me if en == "pe" else -1
        for d in writes:
            if d.w is not None:
                s, v = d.w
                if s != skip and need.get(s, 0) < v:
                    need[s] = v
            for s, v in d.r.items():
                if s != skip and need.get(s, 0) < v:
                    need[s] = v
        self._wait(e, need)
        inst = fn()
        self.nops += 1
        if signal:
            e.tick += 1
            inst.then_inc(self.sems[me], 1)
            e.pending = False
            t = e.tick
        else:
            e.pending = True
            t = e.tick + 1
        for d in reads:
            if d.r.get(me, 0) < t:
                d.r[me] = t
        for d in writes:
            d.w = (me, t)
            d.r = {}
        return inst

    def dma(self, en, out, in_, sem, reads=(), writes=(), **kw):
        if not self.enabled:
            return None
        e = self.eng[en]
        need = {}
        for d in reads:
            if d.w is not None:
                s, v = d.w
                if need.get(s, 0) < v:
                    need[s] = v
        for d in writes:
            if d.w is not None:
                s, v = d.w
                if need.get(s, 0) < v:
                    need[s] = v
            for s, v in d.r.items():
                if need.get(s, 0) < v:
                    need[s] = v
        self._wait(e, need)
        inst = e.be.dma_start(out=out, in_=in_, **kw)
        self.semcum[sem] += 16
        inst.then_inc(self.sems[sem], 16)
        t = self.semcum[sem]
        for d in reads:
            if d.r.get(sem, 0) < t:
                d.r[sem] = t
        for d in writes:
            d.w = (sem, t)
            d.r = {}
        return inst

    def wait_all(self, en, deps):
        e = self.eng[en]
        need = {}
        for d in deps:
            if d.w is not None:
                s, v = d.w
                if need.get(s, 0) < v:
                    need[s] = v
            for s, v in d.r.items():
                if need.get(s, 0) < v:
                    need[s] = v
        self._wait(e, need)

    def barrier(self, dma_sems=()):
        names = ["pe", "act", "dve", "pool"]
        for e in self.eng.values():
            need = {}
            for n in names:
                o = self.eng[n]
                assert not o.pending
                if o is not e and o.tick > 0:
                    need[o.semidx] = o.tick
            for s in dma_sems:
                if self.semcum[s] > 0:
                    need[s] = self.semcum[s]
            self._wait(e, need)


def tsl(start, step):
    return slice(start, start + 127 * step + 1, step)


def build_program(nseq=SEQ_PER_CORE, depth=DEPTH, do_ffn=True, final_norm=True, out_src="x", nblocks_cast=None):
    nc = bass.Bass("TRN2", target_bir_lowering=False)
    x_d = nc.dram_tensor("x", [nseq, S, D], F32, kind="ExternalInput").ap()
    wbig_d = nc.dram_tensor("wbig", [DEPTH * NBLK * 128, 1024], F32, kind="ExternalInput").ap()
    sp_d = nc.dram_tensor("sp", [128, DEPTH * NSP], F32, kind="ExternalInput").ap()
    fg_d = nc.dram_tensor("fg", [128, 8], F32, kind="ExternalInput").ap()
    lruw_d = nc.dram_tensor("lruw", [128, DEPTH * 6 * 128], F32, kind="ExternalInput").ap()
    ident_d = nc.dram_tensor("ident", [128, 128], F32, kind="ExternalInput").ap()
    bias_d = nc.dram_tensor("abias", [128, 3 * 512], F32, kind="ExternalInput").ap()
    y_d = nc.dram_tensor("y", [nseq, S, D], F32, kind="ExternalOutput").ap()
    wscl = [nc.dram_tensor("wsc%d" % l, [NBLK * 128, 1024], BF16, kind="Internal").ap() for l in range(DEPTH)]

    with ExitStack() as es:
        fw = FW(nc, es)

        _tn = [0]

        def T(stack, name, shape, dt):
            _tn[0] += 1
            return stack.enter_context(nc.sbuf_tensor("%s_%d" % (name, _tn[0]), shape, dt))

        banks = [es.enter_context(nc.psum_tensor("bank%d" % i, [128, 512], F32)) for i in range(8)]
        bdep = [Dep(excl=True) for _ in range(8)]
        bank_rr = [0]

        def nb():
            b = bank_rr[0]
            bank_rr[0] = (b + 1) % 8
            return b

        xT = T(es, "xT", [128, 8, S], F32)
        xdep = [[Dep() for _ in range(4)] for _ in range(8)]
        hT = T(es, "hT", [128, 8, S], BF16)
        hdep = [[Dep() for _ in range(4)] for _ in range(8)]
        ident = T(es, "ident", [128, 128], F32)
        onesf = T(es, "onesf", [128, 128], F32)
        onesb = T(es, "onesb", [128, 128], BF16)
        abias = T(es, "abias", [128, 3, 512], F32)
        spt = T(es, "spt", [128, DEPTH, NSP], F32)
        fgt = T(es, "fgt", [128, 8], F32)
        lruw = T(es, "lruw", [128, DEPTH * 6, 128], BF16)
        cct = T(es, "cct", [128, DEPTH, 3], F32)
        negM = T(es, "negM", [128, 1], F32)
        mx = T(es, "mx", [128, 24], F32)
        qk2 = T(es, "qk2", [128, 4], F32)
        n_acc = T(es, "n_acc", [128, 512], F32)
        n_sq = [T(es, "n_sq%d" % i, [128, 512], F32) for i in range(2)]
        n_rt = T(es, "n_rt", [128, 512], F32)
        cdep = Dep()
        d_negM = Dep()
        d_mx = Dep()
        d_qk2 = Dep()
        d_nacc = Dep()
        d_nsq = [Dep(), Dep()]
        d_nrt = Dep()

        s_const = fw.new_sem("const")

        fw.dma("sp", ident[:], ident_d, s_const, writes=[cdep])
        fw.dma("sp", abias[:].rearrange("p a b -> p (a b)"), bias_d, s_const, writes=[cdep])
        fw.dma("sp", spt[:].rearrange("p a b -> p (a b)"), sp_d, s_const, writes=[cdep])
        fw.dma("sp", fgt[:], fg_d, s_const, writes=[cdep])
        fw.op("pool", lambda: nc.gpsimd.memset(onesf[:], 1.0), writes=[cdep])
        fw.op("pool", lambda: nc.gpsimd.memset(onesb[:], 1.0), writes=[cdep])

        with ExitStack() as ph:
            NS = 4
            stf = [T(ph, "stf%d" % i, [128, 1024], F32) for i in range(NS)]
            stb = [T(ph, "stb%d" % i, [128, 1024], BF16) for i in range(NS)]
            dstf = [Dep() for _ in range(NS)]
            dstb = [Dep() for _ in range(NS)]
            s_ld = [fw.new_sem("cl%d" % i) for i in range(NS)]
            s_st = [fw.new_sem("cs%d" % i) for i in range(NS)]
            lw32 = T(ph, "lw32", [128, DEPTH * 6 * 128], F32)
            dlw = Dep()
            s_lw = fw.new_sem("lw")
            fw.dma("sp", lw32[:], lruw_d, s_lw, writes=[dlw])
            fw.op("dve", lambda: nc.vector.tensor_copy(lruw[:].rearrange("p a b -> p (a b)"), lw32[:]),
                  reads=[dlw], writes=[cdep])
            cc_e = T(ph, "cc_e", [128, DEPTH, 3], F32)
            dcc = Dep()
            for l in range(DEPTH):
                lam = spt[:, l, SP_LAM:SP_LAM + 3]
                fw.op("act", lambda: nc.scalar.activation(cc_e[:, l, :], lam, AF.Exp, scale=-1.0),
                      reads=[cdep], writes=[dcc])
                fw.op("act", lambda: nc.scalar.activation(cc_e[:, l, :], cc_e[:, l, :], AF.Ln, bias=1.0),
                      reads=[dcc], writes=[dcc])
                fw.op("dve", lambda: nc.vector.tensor_scalar(cct[:, l, :], cc_e[:, l, :], -8.0, None, ALU.mult),
                      reads=[dcc], writes=[cdep])
            cast_eng = ["dve", "pool", "act"]
            nblocks = DEPTH * NBLK if nblocks_cast is None else nblocks_cast
            for b in range(nblocks):
                sl = b % NS
                rows = slice(b * 128, (b + 1) * 128)
                fw.dma("sp", stf[sl][:], wbig_d[rows, :], s_ld[sl], writes=[dstf[sl]])
                ce = cast_eng[b % 3]
                if ce == "dve":
                    fw.op("dve", lambda: nc.vector.tensor_copy(stb[sl][:], stf[sl][:]), reads=[dstf[sl]], writes=[dstb[sl]])
                elif ce == "pool":
                    fw.op("pool", lambda: nc.gpsimd.tensor_copy(stb[sl][:], stf[sl][:]), reads=[dstf[sl]], writes=[dstb[sl]])
                else:
                    fw.op("act", lambda: nc.scalar.copy(stb[sl][:], stf[sl][:]), reads=[dstf[sl]], writes=[dstb[sl]])
                bl = b % NBLK
                fw.dma("sp", wscl[b // NBLK][bl * 128:(bl + 1) * 128, :], stb[sl][:], s_st[sl], reads=[dstb[sl]])
            fw.barrier(dma_sems=s_st + [s_const, s_lw])

        def tok(tg):
            return slice(tg * 512, (tg + 1) * 512)

        def rmsnorm_to_hT(gain_ap, emit_out):
            for tg in range(4):
                t = tok(tg)
                for c in range(8):
                    if c == 0:
                        fw.op("act", lambda: nc.scalar.activation(n_acc[:], xT[:, c, t], AF.Square),
                              reads=[xdep[c][tg]], writes=[d_nacc])
                    else:
                        sq, dsq = n_sq[c % 2], d_nsq[c % 2]
                        fw.op("act", lambda: nc.scalar.activation(sq[:], xT[:, c, t], AF.Square),
                              reads=[xdep[c][tg]], writes=[dsq])
                        fw.op("pool", lambda: nc.gpsimd.tensor_tensor(n_acc[:], n_acc[:], sq[:], ALU.add),
                              reads=[d_nacc, dsq], writes=[d_nacc])
                b = nb()
                fw.op("pe", lambda: nc.tensor.matmul(banks[b][:], onesf[:], n_acc[:], start=True, stop=True),
                      reads=[d_nacc, cdep], writes=[bdep[b]])
                fw.op("act", lambda: nc.scalar.activation(n_rt[:], banks[b][:], AF.Sqrt, bias=RMS_EPS, scale=1.0 / D),
                      reads=[bdep[b]], writes=[d_nrt])
                fw.op("dve", lambda: nc.vector.reciprocal(n_rt[:], n_rt[:]), reads=[d_nrt], writes=[d_nrt])
                emit_out(tg)

        def norm_hT(gain):
            def out(tg):
                t = tok(tg)
                for c in range(8):
                    fw.op("dve", lambda: nc.vector.scalar_tensor_tensor(
                        hT[:, c, t], xT[:, c, t], gain[:, c:c + 1], n_rt[:], ALU.mult, ALU.mult),
                        reads=[xdep[c][tg], d_nrt, cdep], writes=[hdep[c][tg]])
            rmsnorm_to_hT(gain, out)

        class WStream:
            def __init__(self, stack, name, nslots, width, items):
                self.tiles = [T(stack, "%s%d" % (name, i), [128, width], BF16) for i in range(nslots)]
                self.deps = [Dep() for _ in range(nslots)]
                self.sems = [sem_pool("%s%d" % (name, i)) for i in range(nslots)]
                self.items = items
                self.nslots = nslots
                self.issued = 0
                self.width = width

            def _issue(self, i):
                sl = i % self.nslots
                row0, rpp = self.items[i]
                src = wscl[row0 // (NBLK * 128)][row0 % (NBLK * 128):row0 % (NBLK * 128) + 128 * rpp, :]
                if rpp > 1:
                    src = src.rearrange("(p r) c -> p (r c)", r=rpp)
                fw.dma("sp", self.tiles[sl][:], src, self.sems[sl], writes=[self.deps[sl]])

            def get(self, i):
                while self.issued < min(len(self.items), i + self.nslots):
                    self._issue(self.issued)
                    self.issued += 1
                sl = i % self.nslots
                return self.tiles[sl], self.deps[sl]

        _sem_pool = {}

        def sem_pool(name):
            if name not in _sem_pool:
                _sem_pool[name] = fw.new_sem(name)
            return _sem_pool[name]

        def evac_alt(i):
            return "act" if i % 2 == 0 else "dve"

        def copy_op(en, out, in_):
            if en == "act":
                return lambda: nc.scalar.copy(out, in_)
            return lambda: nc.vector.tensor_copy(out, in_)

        for s in range(nseq):
            with ExitStack() as ph:
                xst = [T(ph, "xst%d" % i, [128, 4, D], F32) for i in range(2)]
                dxst = [Dep(), Dep()]
                s_x = [sem_pool("xs%d" % i) for i in range(2)]
                ev = 0
                for tg in range(4):
                    sl = tg % 2
                    src = x_d[s, tg * 512:(tg + 1) * 512, :].rearrange("(t p) f -> p t f", p=128)
                    fw.dma("sp", xst[sl][:], src, s_x[sl], writes=[dxst[sl]])
                    for c in range(8):
                        b = nb()
                        for tt in range(4):
                            fw.op("pe", lambda: nc.tensor.transpose(banks[b][:, tt * 128:(tt + 1) * 128],
                                                                     xst[sl][:, tt, c * 128:(c + 1) * 128], ident[:]),
                                  reads=[dxst[sl], cdep], writes=[bdep[b]], signal=(tt == 3))
                        en = evac_alt(ev)
                        ev += 1
                        fw.op(en, copy_op(en, xT[:, c, tok(tg)], banks[b][:]), reads=[bdep[b]], writes=[xdep[c][tg]])
                fw.barrier()

            for l in range(depth):
                rbase = l * NBLK * 128
                gsp = spt[:, l, :]
                with ExitStack() as pa:
                    mixT = T(pa, "mixT", [128, 8, S], BF16)
                    mdep = [[Dep() for _ in range(4)] for _ in range(8)]
                    order = [0, 1, 2, 3, 4, 5, 6, 7, 8, 11, 9, 12, 10, 13, 16, 14, 17, 15, 18] + [BLK_OUT + m for m in range(8)]
                    ws = WStream(pa, "wa", 3, 1024, [(rbase + j * 128, 1) for j in order])
                    wi = [0]

                    def next_w():
                        t_, d_ = ws.get(wi[0])
                        wi[0] += 1
                        return t_[:].rearrange("p (k n) -> p k n", k=8), d_

                    norm_hT(gsp[:, SP_G1:SP_G1 + 8])

                    def zchunk(evac):
                        wt, wd = next_w()
                        bs = [nb() for _ in range(4)]
                        for kc in range(8):
                            for tg in range(4):
                                b = bs[tg]
                                fw.op("pe", lambda: nc.tensor.matmul(banks[b][:], wt[:, kc, :], hT[:, kc, tok(tg)],
                                                                     start=(kc == 0), stop=(kc == 7)),
                                      reads=[wd, hdep[kc][tg]], writes=[bdep[b]], signal=(kc == 7))
                        for tg in range(4):
                            evac(tg, bs[tg])

                    import os as _os
                    _en = _os.environ.get("KPH", "acl")
                    with ExitStack() as pb:
                      if "a" in _en:
                            qk = T(pb, "qk", [128, 6, S], BF16)
                            dqk = [Dep() for _ in range(6)]
                            vtok = T(pb, "vtok", [128, 3, 16, 128], BF16)
                            dv = [[Dep() for _ in range(4)] for _ in range(3)]
                            zb = T(pb, "zb", [128, S], F32)
                            dzb = [Dep() for _ in range(4)]
                            tmp = [T(pb, "atmp%d" % i, [128, 512], F32) for i in range(2)]
                            dtmp = [Dep(), Dep()]
                            pt = [T(pb, "apt%d" % i, [128, 512], BF16) for i in range(2)]
                            dpt = [Dep(), Dep()]
                            sqb = [T(pb, "sqb%d" % i, [128, 512], BF16) for i in range(2)]
                            dsqb = [Dep(), Dep()]
                            cnt = [0]

                            _at = _os.environ.get("KAT", "1234")
                            fw.enabled = "1" in _at
                            for ci in range(6):
                                def ev_qk(tg, b, ci=ci):
                                    t = tok(tg)
                                    fw.op("act", lambda: nc.scalar.copy(qk[:, ci, t], banks[b][:]),
                                          reads=[bdep[b]], writes=[dqk[ci]])
                                    i2 = cnt[0] % 2
                                    cnt[0] += 1
                                    fw.op("pool", lambda: nc.gpsimd.tensor_tensor(sqb[i2][:], qk[:, ci, t], qk[:, ci, t], ALU.mult),
                                          reads=[dqk[ci]], writes=[dsqb[i2]])
                                    b2 = nb()
                                    fw.op("pe", lambda: nc.tensor.matmul(banks[b2][:], onesb[:], sqb[i2][:], start=True, stop=True),
                                          reads=[dsqb[i2], cdep], writes=[bdep[b2]])
                                    col = ci * 4 + tg
                                    fw.op("dve", lambda: nc.vector.tensor_reduce(mx[:, col:col + 1], banks[b2][:], AX.X, ALU.max),
                                          reads=[bdep[b2]], writes=[d_mx])
                                zchunk(ev_qk)
                            fw.op("dve", lambda: nc.vector.tensor_reduce(qk2[:, 0:1], mx[:, 0:12], AX.X, ALU.max),
                                  reads=[d_mx], writes=[d_qk2])
                            fw.op("dve", lambda: nc.vector.tensor_reduce(qk2[:, 1:2], mx[:, 12:24], AX.X, ALU.max),
                                  reads=[d_mx], writes=[d_qk2])
                            fw.op("dve", lambda: nc.vector.tensor_tensor(qk2[:, 2:3], qk2[:, 0:1], qk2[:, 1:2], ALU.mult),
                                  reads=[d_qk2], writes=[d_qk2])
                            fw.op("act", lambda: nc.scalar.activation(qk2[:, 3:4], qk2[:, 2:3], AF.Sqrt),
                                  reads=[d_qk2], writes=[d_qk2])
                            fw.op("dve", lambda: nc.vector.tensor_scalar(negM[:], qk2[:, 3:4], -0.125 * 1.05, None, ALU.mult),
                                  reads=[d_qk2], writes=[d_negM])

                            fw.enabled = "2" in _at
                            for g in range(3):
                                d = DIL[g]
                                nbk = 16 // d
                                wt, wd = next_w()
                                for t4 in range(4):
                                    b = nb()
                                    for tj in range(4):
                                        ti = t4 * 4 + tj
                                        r, n = ti // nbk, ti % nbk
                                        st0 = r + d * 128 * n
                                        if g == 0:
                                            tgs = [n // 4]
                                        elif g == 1:
                                            tgs = [n]
                                        else:
                                            tgs = [0, 1, 2, 3]
                                        for kc in range(8):
                                            fw.op("pe", lambda: nc.tensor.matmul(banks[b][:, tj * 128:(tj + 1) * 128],
                                                                                 hT[:, kc, tsl(st0, d)], wt[:, kc, :],
                                                                                 start=(kc == 0), stop=(kc == 7)),
                                                  reads=[wd] + [hdep[kc][x] for x in tgs], writes=[bdep[b]],
                                                  signal=(kc == 7 and tj == 3))
                                    fw.op("dve", lambda: nc.vector.tensor_copy(
                                        vtok[:, g, t4 * 4:(t4 + 1) * 4, :].rearrange("p a b -> p (a b)"), banks[b][:]),
                                        reads=[bdep[b]], writes=[dv[g][t4]])

                            fw.enabled = "3" in _at
                            it = 0
                            for g in range(3):
                                d = DIL[g]
                                nbk = 16 // d
                                for ti in range(16):
                                    r, n = ti // nbk, ti % nbk
                                    qs = r + d * 128 * n
                                    if g == 0:
                                        tgs = [n // 4]
                                    elif g == 1:
                                        tgs = [n]
                                    else:
                                        tgs = [0, 1, 2, 3]
                                    kbs = [1] if n == 0 else [0, 1]
                                    bsth = [nb(), nb()]
                                    i2 = it % 2
                                    it += 1
                                    cols = slice(0, 512) if n > 0 else slice(256, 512)
                                    for h in range(2):
                                        bst = bsth[h]
                                        for ik, kb in enumerate(kbs):
                                            kti = ti - 1 if kb == 0 else ti
                                            ks = (kti // nbk) + d * 128 * (kti % nbk)
                                            fw.op("pe", lambda: nc.tensor.matmul(
                                                banks[bst][:, kb * 128:(kb + 1) * 128],
                                                qk[64 * h:64 * h + 64, 3 + g, tsl(ks, d)],
                                                qk[64 * h:64 * h + 64, g, tsl(qs, d)], start=True, stop=True),
                                                reads=[dqk[g], dqk[3 + g]], writes=[bdep[bst]], signal=(ik == len(kbs) - 1))
                                    for h in range(2):
                                        bst = bsth[h]
                                        k0 = kbs[0]
                                        nk = len(kbs)
                                        tview = tmp[i2][:].rearrange("p (a b c) -> p a b c", a=2, b=2)[:, k0:2, h, :]
                                        bview = abias[:, g, :].rearrange("p (a b c) -> p a b c", a=2, b=2)[:, k0:2, h, :]
                                        sview = banks[bst][:, k0 * 128:256].rearrange("p (a c) -> p a c", a=nk)
                                        fw.op("dve", lambda: nc.vector.scalar_tensor_tensor(
                                            tview, sview, 0.125, bview, ALU.mult, ALU.add),
                                            reads=[bdep[bst], cdep], writes=[dtmp[i2]])
                                    fw.op("act", lambda: nc.scalar.activation(pt[i2][:, cols], tmp[i2][:, cols], AF.Exp, bias=negM[:, 0:1]),
                                          reads=[dtmp[i2], d_negM], writes=[dpt[i2]])
                                    bu = nb()
                                    vdeps = [dv[g][ti // 4]] + ([dv[g][(ti - 1) // 4]] if n > 0 else [])
                                    for part in range(2):
                                        for h in range(2):
                                            for idx, kb in enumerate(kbs):
                                                kti = ti - 1 if kb == 0 else ti
                                                o0 = (kb * 2 + h) * 128
                                                lhs = vtok[:, g, kti, 64 * h:64 * h + 64] if part == 0 else onesb[:, 0:64]
                                                last = (part == 1 and h == 1 and idx == len(kbs) - 1)
                                                fw.op("pe", lambda: nc.tensor.matmul(
                                                    banks[bu][64 * h:64 * h + 64, part * 128:(part + 1) * 128],
                                                    lhs, pt[i2][:, o0:o0 + 128],
                                                    start=(idx == 0), stop=(idx == len(kbs) - 1)),
                                                    reads=[dpt[i2], cdep] + vdeps, writes=[bdep[bu]], signal=last)
                                    fw.op("act", lambda: nc.scalar.copy(mixT[:, g, tsl(qs, d)], banks[bu][:, 0:128]),
                                          reads=[bdep[bu]], writes=[mdep[g][x] for x in tgs])
                                    if g == 0:
                                        fw.op("dve", lambda: nc.vector.tensor_copy(zb[:, tsl(qs, d)], banks[bu][:, 128:256]),
                                              reads=[bdep[bu]], writes=[dzb[x] for x in tgs])
                                    else:
                                        fw.op("dve", lambda: nc.vector.tensor_tensor(
                                            zb[:, tsl(qs, d)], banks[bu][:, 128:256], zb[:, tsl(qs, d)], ALU.add),
                                            reads=[bdep[bu]] + [dzb[x] for x in tgs], writes=[dzb[x] for x in tgs])
                            fw.enabled = "4" in _at
                            for tg in range(4):
                                t = tok(tg)
                                fw.op("dve", lambda: nc.vector.reciprocal(zb[:, t], zb[:, t]), reads=[dzb[tg]], writes=[dzb[tg]])
                                for g in range(3):
                                    fw.op("dve", lambda: nc.vector.tensor_tensor(mixT[:, g, t], mixT[:, g, t], zb[:, t], ALU.mult),
                                          reads=[mdep[g][tg], dzb[tg]], writes=[mdep[g][tg]])
                            fw.barrier()

                    fw.enabled = True
                    with ExitStack() as pb:
                      if "c" in _en:
                            PADC = 30
                            u = T(pb, "cu", [128, 2, PADC + S], F32)
                            du = [[Dep() for _ in range(4)] for _ in range(2)]
                            acc = T(pb, "cacc", [128, 2, S], F32)
                            dacc = [[Dep() for _ in range(2)] for _ in range(2)]
                            sg = T(pb, "csg", [128, S], F32)
                            dsg = [Dep() for _ in range(4)]
                            csq = [T(pb, "csq%d" % i, [128, 512], F32) for i in range(2)]
                            dcsq = [Dep(), Dep()]
                            cmean = T(pb, "cmean", [128, 512], F32)
                            cmsq = T(pb, "cmsq", [128, 512], F32)
                            cvar = T(pb, "cvar", [128, 512], F32)
                            cta = T(pb, "cta", [128, 512], F32)
                            ctb = T(pb, "ctb", [128, 512], F32)
                            dmean, dmsq, dvar, dta, dtb = Dep(), Dep(), Dep(), Dep(), Dep()
                            dpad = Dep()
                            fw.op("pool", lambda: nc.gpsimd.memset(u[:, :, 0:PADC], 0.0), writes=[dpad])
                            for c in range(2):
                                def ev_g(tg, b):
                                    fw.op("act", lambda: nc.scalar.activation(sg[:, tok(tg)], banks[b][:], AF.Sigmoid),
                                          reads=[bdep[b]], writes=[dsg[tg]])
                                zchunk(ev_g)

                                def ev_a(tg, b, c=c):
                                    fw.op("dve", lambda: nc.vector.tensor_tensor(
                                        u[:, c, PADC + tg * 512:PADC + (tg + 1) * 512], banks[b][:], sg[:, tok(tg)], ALU.mult),
                                        reads=[bdep[b], dsg[tg]], writes=[du[c][tg]])
                                zchunk(ev_a)
                            for c in range(2):
                                for k in range(31):
                                    for hh in range(2):
                                        o = acc[:, c, hh * 1024:(hh + 1) * 1024]
                                        i0 = u[:, c, k + hh * 1024:k + hh * 1024 + 1024]
                                        rd = [du[c][x] for x in range(4)] + [dpad, cdep]
                                        wcol = gsp[:, SP_CW + c * 31 + k:SP_CW + c * 31 + k + 1]
                                        if k == 0:
                                            fw.op("dve", lambda: nc.vector.tensor_scalar(
                                                o, i0, wcol, gsp[:, SP_CB + c:SP_CB + c + 1], ALU.mult, ALU.add),
                                                reads=rd, writes=[dacc[c][hh]])
                                        else:
                                            fw.op("dve", lambda: nc.vector.scalar_tensor_tensor(
                                                o, i0, wcol, o, ALU.mult, ALU.add),
                                                reads=rd + [dacc[c][hh]], writes=[dacc[c][hh]])
                            for tg in range(4):
                                t = tok(tg)
                                hh = tg // 2
                                b1 = nb()
                                for c in range(2):
                                    fw.op("pe", lambda: nc.tensor.matmul(banks[b1][:], onesf[:], acc[:, c, t], start=(c == 0), stop=(c == 1)),
                                          reads=[dacc[c][hh], cdep], writes=[bdep[b1]], signal=(c == 1))
                                for c in range(2):
                                    fw.op("act", lambda: nc.scalar.activation(csq[c][:], acc[:, c, t], AF.Square),
                                          reads=[dacc[c][hh]], writes=[dcsq[c]])
                                b2 = nb()
                                for c in range(2):
                                    fw.op("pe", lambda: nc.tensor.matmul(banks[b2][:], onesf[:], csq[c][:], start=(c == 0), stop=(c == 1)),
                                          reads=[dcsq[c], cdep], writes=[bdep[b2]], signal=(c == 1))
                                fw.op("dve", lambda: nc.vector.tensor_scalar(cmean[:], banks[b1][:], 1.0 / 256, None, ALU.mult),
                                      reads=[bdep[b1]], writes=[dmean])
                                fw.op("dve", lambda: nc.vector.tensor_tensor(cmsq[:], cmean[:], cmean[:], ALU.mult),
                                      reads=[dmean], writes=[dmsq])
                                fw.op("dve", lambda: nc.vector.scalar_tensor_tensor(cvar[:], banks[b2][:], 1.0 / 256, cmsq[:], ALU.mult, ALU.subtract),
                                      reads=[bdep[b2], dmsq], writes=[dvar])
                                fw.op("act", lambda: nc.scalar.activation(cvar[:], cvar[:], AF.Sqrt, bias=LN_EPS),
                                      reads=[dvar], writes=[dvar])
                                fw.op("dve", lambda: nc.vector.reciprocal(cvar[:], cvar[:]), reads=[dvar], writes=[dvar])
                                for c in range(2):
                                    fw.op("dve", lambda: nc.vector.tensor_tensor(cta[:], acc[:, c, t], cmean[:], ALU.subtract),
                                          reads=[dacc[c][hh], dmean], writes=[dta])
                                    fw.op("dve", lambda: nc.vector.tensor_tensor(ctb[:], cta[:], cvar[:], ALU.mult),
                                          reads=[dta, dvar], writes=[dtb])
                                    fw.op("act", lambda: nc.scalar.activation(
                                        mixT[:, 3 + c, t], ctb[:], AF.Silu,
                                        bias=gsp[:, SP_LNB + c:SP_LNB + c + 1], scale=gsp[:, SP_LNG + c:SP_LNG + c + 1]),
                                        reads=[dtb, cdep], writes=[mdep[3 + c][tg]])
                            fw.barrier()

                    with ExitStack() as pb:
                      if "l" in _en:
                            PADL = 3
                            gg = T(pb, "lgg", [128, S], F32)
                            xr = T(pb, "lxr", [128, PADL + S], F32)
                            ul = T(pb, "lul", [128, S], F32)
                            ub = T(pb, "lub", [128, S], BF16)
                            r_ = T(pb, "lr", [128, S], F32)
                            i_ = T(pb, "li", [128, S], F32)
                            t1 = T(pb, "lt1", [128, S], F32)
                            dgg = [Dep() for _ in range(4)]
                            dxr = [Dep() for _ in range(4)]
                            dpad = Dep()
                            dul = [Dep(), Dep()]
                            dub = [Dep(), Dep()]
                            dr = [Dep() for _ in range(4)]
                            di = [Dep() for _ in range(4)]
                            dt1 = [Dep() for _ in range(4)]
                            fw.op("pool", lambda: nc.gpsimd.memset(xr[:, 0:PADL], 0.0), writes=[dpad])
                            for c in range(3):
                                def ev_gate(tg, b):
                                    fw.op("act", lambda: nc.scalar.activation(gg[:, tok(tg)], banks[b][:], AF.Gelu_apprx_tanh),
                                          reads=[bdep[b]], writes=[dgg[tg]])
                                zchunk(ev_gate)

                                def ev_x(tg, b):
                                    fw.op("dve", lambda: nc.vector.tensor_copy(xr[:, PADL + tg * 512:PADL + (tg + 1) * 512], banks[b][:]),
                                          reads=[bdep[b]], writes=[dxr[tg]])
                                zchunk(ev_x)
                                for k in range(4):
                                    for hh in range(2):
                                        o = ul[:, hh * 1024:(hh + 1) * 1024]
                                        i0 = xr[:, k + hh * 1024:k + hh * 1024 + 1024]
                                        rd = dxr + [dpad, cdep]
                                        wcol = gsp[:, SP_LCW + c * 4 + k:SP_LCW + c * 4 + k + 1]
                                        if k == 0:
                                            fw.op("dve", lambda: nc.vector.tensor_scalar(
                                                o, i0, wcol, gsp[:, SP_LCB + c:SP_LCB + c + 1], ALU.mult, ALU.add),
                                                reads=rd, writes=[dul[hh]])
                                        else:
                                            fw.op("dve", lambda: nc.vector.scalar_tensor_tensor(o, i0, wcol, o, ALU.mult, ALU.add),
                                                  reads=rd + [dul[hh]], writes=[dul[hh]])
                                for hh in range(2):
                                    fw.op("pool", lambda: nc.gpsimd.tensor_copy(ub[:, hh * 1024:(hh + 1) * 1024], ul[:, hh * 1024:(hh + 1) * 1024]),
                                          reads=[dul[hh]], writes=[dub[hh]])
                                for tg in range(4):
                                    t = tok(tg)
                                    hh = tg // 2
                                    ba_, bx_ = nb(), nb()
                                    fw.op("pe", lambda: nc.tensor.matmul(banks[ba_][:], lruw[:, l * 6 + c * 2 + 0, :], ub[:, t], start=True, stop=True),
                                          reads=[dub[hh], cdep], writes=[bdep[ba_]])
                                    fw.op("pe", lambda: nc.tensor.matmul(banks[bx_][:], lruw[:, l * 6 + c * 2 + 1, :], ub[:, t], start=True, stop=True),
                                          reads=[dub[hh], cdep], writes=[bdep[bx_]])
                                    fw.op("act", lambda: nc.scalar.activation(r_[:, t], banks[ba_][:], AF.Sigmoid, bias=gsp[:, SP_LBA + c:SP_LBA + c + 1]),
                                          reads=[bdep[ba_], cdep], writes=[dr[tg]])
                                    fw.op("act", lambda: nc.scalar.activation(i_[:, t], banks[bx_][:], AF.Sigmoid, bias=gsp[:, SP_LBX + c:SP_LBX + c + 1]),
                                          reads=[bdep[bx_], cdep], writes=[di[tg]])
                                for tg in range(4):
                                    t = tok(tg)
                                    hh = tg // 2
                                    fw.op("act", lambda: nc.scalar.activation(r_[:, t], r_[:, t], AF.Exp, scale=cct[:, l, c:c + 1]),
                                          reads=[dr[tg], cdep], writes=[dr[tg]])
                                    fw.op("dve", lambda: nc.vector.tensor_tensor(t1[:, t], r_[:, t], r_[:, t], ALU.mult),
                                          reads=[dr[tg]], writes=[dt1[tg]])
                                    fw.op("act", lambda: nc.scalar.activation(t1[:, t], t1[:, t], AF.Sqrt, bias=1.0, scale=-1.0),
                                          reads=[dt1[tg]], writes=[dt1[tg]])
                                    fw.op("dve", lambda: nc.vector.tensor_tensor(i_[:, t], i_[:, t], ul[:, t], ALU.mult),
                                          reads=[di[tg], dul[hh]], writes=[di[tg]])
                                    fw.op("dve", lambda: nc.vector.tensor_tensor(i_[:, t], i_[:, t], t1[:, t], ALU.mult),
                                          reads=[di[tg], dt1[tg]], writes=[di[tg]])
                                fw.op("dve", lambda: nc.vector.tensor_tensor_scan(t1[:], r_[:], i_[:], 0.0, ALU.mult, ALU.add),
                                      reads=dr + di + dt1, writes=dt1)
                                for tg in range(4):
                                    t = tok(tg)
                                    fw.op("dve", lambda: nc.vector.tensor_tensor(mixT[:, 5 + c, t], gg[:, t], t1[:, t], ALU.mult),
                                          reads=[dgg[tg], dt1[tg]], writes=[mdep[5 + c][tg]])
                            fw.barrier()

                    if out_src == "mix" and l == depth - 1:
                        for c in range(8):
                            for tg in range(4):
                                fw.op("dve", lambda: nc.vector.tensor_copy(xT[:, c, tok(tg)], mixT[:, c, tok(tg)]),
                                      reads=[mdep[c][tg]], writes=[xdep[c][tg]])
                    else:
                        for m in range(8):
                            wt, wd = next_w()
                            bs = [nb() for _ in range(4)]
                            for kc in range(8):
                                for tg in range(4):
                                    b = bs[tg]
                                    fw.op("pe", lambda: nc.tensor.matmul(banks[b][:], wt[:, kc, :], mixT[:, kc, tok(tg)],
                                                                         start=(kc == 0), stop=(kc == 7)),
                                          reads=[wd, mdep[kc][tg]], writes=[bdep[b]], signal=(kc == 7))
                            for tg in range(4):
                                b = bs[tg]
                                fw.op("dve", lambda: nc.vector.tensor_tensor(xT[:, m, tok(tg)], banks[b][:], xT[:, m, tok(tg)], ALU.add),
                                      reads=[bdep[b], xdep[m][tg]], writes=[xdep[m][tg]])
                    fw.barrier(dma_sems=ws.sems)

                if do_ffn and not (out_src == "mix" and l == depth - 1):
                    with ExitStack() as pa:
                        aT = T(pa, "aT", [128, 32, 1024], BF16)
                        da = [[Dep() for _ in range(2)] for _ in range(32)]
                        rl = [T(pa, "rl%d" % i, [128, 512], F32) for i in range(2)]
                        drl = [Dep() for _ in range(2)]
                        wsu = WStream(pa, "wu", 3, 1024, [(rbase + (BLK_UP + f) * 128, 1) for f in range(32)] * 2)
                        wsd = WStream(pa, "wd", 2, 4096, [(rbase + (BLK_DOWN + 4 * m) * 128, 4) for m in range(8)] * 2)
                        norm_hT(gsp[:, SP_G2:SP_G2 + 8])
                        iu = 0
                        idn = 0
                        irl = 0
                        for t2 in range(2):
                            for f in range(32):
                                wt_, wd = wsu.get(iu)
                                iu += 1
                                wt = wt_[:].rearrange("p (k n) -> p k n", k=8)
                                bs = [nb(), nb()]
                                for kc in range(8):
                                    for hf in range(2):
                                        tg = t2 * 2 + hf
                                        b = bs[hf]
                                        fw.op("pe", lambda: nc.tensor.matmul(banks[b][:], wt[:, kc, :], hT[:, kc, tok(tg)],
                                                                             start=(kc == 0), stop=(kc == 7)),
                                              reads=[wd, hdep[kc][tg]], writes=[bdep[b]], signal=(kc == 7))
                                for hf in range(2):
                                    b = bs[hf]
                                    ir = irl % 2
                                    irl += 1
                                    fw.op("act", lambda: nc.scalar.activation(rl[ir][:], banks[b][:], AF.Relu),
                                          reads=[bdep[b]], writes=[drl[ir]])
                                    fw.op("pool", lambda: nc.gpsimd.tensor_tensor(aT[:, f, hf * 512:(hf + 1) * 512], rl[ir][:], rl[ir][:], ALU.mult),
                                          reads=[drl[ir]], writes=[da[f][hf]])
                            for m in range(8):
                                wt_, wd = wsd.get(idn)
                                idn += 1
                                wt = wt_[:].rearrange("p (k n) -> p k n", k=32)
                                bs = [nb(), nb()]
                                for f in range(32):
                                    for hf in range(2):
                                        b = bs[hf]
                                        fw.op("pe", lambda: nc.tensor.matmul(banks[b][:], wt[:, f, :], aT[:, f, hf * 512:(hf + 1) * 512],
                                                                             start=(f == 0), stop=(f == 31)),
                                              reads=[wd, da[f][hf]], writes=[bdep[b]], signal=(f == 31))
                                for hf in range(2):
                                    tg = t2 * 2 + hf
                                    b = bs[hf]
                                    fw.op("dve", lambda: nc.vector.tensor_tensor(xT[:, m, tok(tg)], banks[b][:], xT[:, m, tok(tg)], ALU.add),
                                          reads=[bdep[b], xdep[m][tg]], writes=[xdep[m][tg]])
                        fw.barrier(dma_sems=wsu.sems + wsd.sems)

            with ExitStack() as ph:
                yst = [T(ph, "yst%d" % i, [128, 4, D], F32) for i in range(2)]
                dyst = [Dep(), Dep()]
                s_y = [sem_pool("ys%d" % i) for i in range(2)]
                ytmp = [T(ph, "ytmp%d" % i, [128, 512], F32) for i in range(2)]
                dytmp = [Dep(), Dep()]
                ev = [0]

                def emit_store(tg):
                    t = tok(tg)
                    sl = tg % 2
                    for c in range(8):
                        i2 = c % 2
                        if final_norm:
                            fw.op("dve", lambda: nc.vector.scalar_tensor_tensor(
                                ytmp[i2][:], xT[:, c, t], fgt[:, c:c + 1], n_rt[:], ALU.mult, ALU.mult),
                                reads=[xdep[c][tg], d_nrt, cdep], writes=[dytmp[i2]])
                            src, sdep = ytmp[i2], dytmp[i2]
                        b = nb()
                        for tt in range(4):
                            if final_norm:
                                in_ = src[:, tt * 128:(tt + 1) * 128]
                                rd = [sdep, cdep]
                            else:
                                in_ = xT[:, c, tg * 512 + tt * 128:tg * 512 + (tt + 1) * 128]
                                rd = [xdep[c][tg], cdep]
                            fw.op("pe", lambda: nc.tensor.transpose(banks[b][:, tt * 128:(tt + 1) * 128], in_, ident[:]),
                                  reads=rd, writes=[bdep[b]], signal=(tt == 3))
                        en = evac_alt(ev[0])
                        ev[0] += 1
                        fw.op(en, copy_op(en, yst[sl][:, :, c * 128:(c + 1) * 128],
                                          banks[b][:].rearrange("p (a b) -> p a b", a=4)),
                              reads=[bdep[b]], writes=[dyst[sl]])
                    dst = y_d[s, tg * 512:(tg + 1) * 512, :].rearrange("(t p) f -> p t f", p=128)
                    fw.dma("sp", dst, yst[sl][:], s_y[sl], reads=[dyst[sl]])

                if final_norm:
                    rmsnorm_to_hT(None, emit_store)
                else:
                    for tg in range(4):
                        emit_store(tg)
                fw.barrier(dma_sems=s_y)
        print("program: ops=%d waits=%d sems=%d" % (fw.nops, fw.nwaits, len(fw.sems)))
    return nc


def prep_shared(inp):
    f32 = np.float32
    per_layer = []
    for l in range(DEPTH):
        w_in = np.asarray(inp["w_in"][l], f32).reshape(8, 128, 19, 128).transpose(2, 1, 0, 3)
        w_out = np.asarray(inp["w_out"][l], f32).reshape(8, 128, 8, 128).transpose(2, 1, 0, 3)
        w_up = np.asarray(inp["w_up"][l], f32).reshape(8, 128, 32, 128).transpose(2, 1, 0, 3)
        w_down = np.asarray(inp["w_down"][l], f32).reshape(32, 128, 8, 128).transpose(2, 1, 0, 3)
        per_layer.append(np.concatenate([w_in.ravel(), w_out.ravel(), w_up.ravel(), w_down.ravel()]))
    wbig = np.ascontiguousarray(np.stack(per_layer).reshape(DEPTH * NBLK * 128, 1024))

    def fm(v, nch):
        return np.asarray(v, f32).reshape(nch, 128).T

    sp = np.zeros((128, DEPTH, NSP), f32)
    lruw = np.zeros((128, DEPTH, 3, 2, 128), f32)
    for l in range(DEPTH):
        sp[:, l, SP_G1:SP_G1 + 8] = fm(inp["norm1_g"][l], 8)
        sp[:, l, SP_G2:SP_G2 + 8] = fm(inp["norm2_g"][l], 8)
        cw = np.asarray(inp["conv_dw_w"][l], f32)
        sp[:, l, SP_CW:SP_CW + 62] = cw.reshape(31, 2, 128).transpose(2, 1, 0).reshape(128, 62)
        sp[:, l, SP_CB:SP_CB + 2] = fm(inp["conv_dw_b"][l], 2)
        sp[:, l, SP_LNG:SP_LNG + 2] = fm(inp["conv_ln_g"][l], 2)
        sp[:, l, SP_LNB:SP_LNB + 2] = fm(inp["conv_ln_b"][l], 2)
        lcw = np.asarray(inp["lru_conv_w"][l], f32)
        sp[:, l, SP_LCW:SP_LCW + 12] = lcw.reshape(4, 3, 128).transpose(2, 1, 0).reshape(128, 12)
        sp[:, l, SP_LCB:SP_LCB + 3] = fm(inp["lru_conv_b"][l], 3)
        sp[:, l, SP_LBA:SP_LBA + 3] = fm(inp["lru_ba"][l], 3)
        sp[:, l, SP_LBX:SP_LBX + 3] = fm(inp["lru_bx"][l], 3)
        sp[:, l, SP_LAM:SP_LAM + 3] = fm(inp["lru_lambda"][l], 3)
        for c in range(3):
            for hh in range(2):
                lruw[64 * hh:64 * hh + 64, l, c, 0, 64 * hh:64 * hh + 64] = inp["lru_wa"][l][2 * c + hh]
                lruw[64 * hh:64 * hh + 64, l, c, 1, 64 * hh:64 * hh + 64] = inp["lru_wx"][l][2 * c + hh]
    fg = fm(inp["final_g"], 8)
    ident = np.eye(128, dtype=f32)
    slopes = (2.0 ** (-8.0 * np.arange(1, 7) / 6)).astype(f32)
    kk = np.arange(128)[:, None]
    qq = np.arange(128)[None, :]
    ab = np.zeros((128, 3, 2, 2, 128), f32)
    for g in range(3):
        for h in range(2):
            sl = slopes[2 * g + h]
            dist_prev = qq + 128 - kk
            dist_cur = qq - kk
            bp = -(sl * (DIL[g] * dist_prev).astype(f32))
            bc = -(sl * (DIL[g] * dist_cur).astype(f32))
            ab[:, g, 0, h, :] = np.where(dist_prev <= 128, bp, MASK_NEG)
            ab[:, g, 1, h, :] = np.where(dist_cur >= 0, bc, MASK_NEG)
    return {
        "wbig": wbig,
        "sp": np.ascontiguousarray(sp.reshape(128, DEPTH * NSP)),
        "fg": np.ascontiguousarray(fg),
        "lruw": np.ascontiguousarray(lruw.reshape(128, DEPTH * 6 * 128)),
        "ident": ident,
        "abias": np.ascontiguousarray(ab.reshape(128, 3 * 512)),
    }


_PROG = {}


def kernel(**inputs):
    x = np.ascontiguousarray(np.asarray(inputs["x"], np.float32))
    shared = prep_shared(inputs)
    if "nc" not in _PROG:
        _PROG["nc"] = build_program()
    nc = _PROG["nc"]
    in_maps = []
    for c in range(NCORES):
        m = dict(shared)
        m["x"] = np.ascontiguousarray(x[c * SEQ_PER_CORE:(c + 1) * SEQ_PER_CORE])
        in_maps.append(m)
    res = run_bass_kernel_spmd(nc, in_maps, core_ids=list(range(NCORES)))
    return np.concatenate([np.asarray(r["y"], np.float32) for r in res.results], axis=0)
```

```python
import numpy as np
from contextlib import ExitStack
import concourse.bass as bass
import concourse.mybir as mybir
from concourse.bass_utils import run_bass_kernel_spmd

F32 = mybir.dt.float32
BF16 = mybir.dt.bfloat16
AF = mybir.ActivationFunctionType
ALU = mybir.AluOpType
AX = mybir.AxisListType

D = 1024
S = 2048
DEPTH = 2
NCORES = 8
BATCH = 32
SEQ_PER_CORE = BATCH // NCORES
DFF = 4096
IN_COLS = 2432
DIL = (1, 4, 16)
NBLK = 91
BLK_OUT = 19
BLK_UP = 27
BLK_DOWN = 59
NSP = 108
SP_G1, SP_G2, SP_CW, SP_CB, SP_LNG, SP_LNB, SP_LCW, SP_LCB, SP_LBA, SP_LBX, SP_LAM = 0, 8, 16, 78, 80, 82, 84, 96, 99, 102, 105
RMS_EPS = 1e-6
LN_EPS = 1e-5
MASK_NEG = -30000.0


class Dep:
    __slots__ = ("w", "r", "excl")

    def __init__(self, excl=False):
        self.w = None
        self.r = {}
        self.excl = excl


class Eng:
    def __init__(self, name, be, semidx):
        self.name = name
        self.be = be
        self.semidx = semidx
        self.tick = 0
        self.waited = {}
        self.pending = False


class FW:
    def __init__(self, nc, es):
        self.nc = nc
        self.es = es
        self.sems = []
        self.semcum = []
        self.eng = {}
        for name, be in (("pe", nc.tensor), ("act", nc.scalar), ("dve", nc.vector),
                         ("pool", nc.gpsimd), ("sp", nc.sync)):
            self.eng[name] = Eng(name, be, self.new_sem("e_" + name))
        self.nwaits = 0
        self.nops = 0

    def new_sem(self, name):
        h = self.es.enter_context(self.nc.semaphore(name))
        self.sems.append(h)
        self.semcum.append(0)
        return len(self.sems) - 1

    def _wait(self, e, need):
        for s, v in need.items():
            if e.waited.get(s, 0) < v:
                e.be.wait_ge(self.sems[s], v)
                e.waited[s] = v
                self.nwaits += 1

    enabled = True

    def op(self, en, fn, reads=(), writes=(), signal=True):
        if not self.enabled:
            return None
        if any(d.excl for d in reads):
            writes = list(writes) + [d for d in reads if d.excl and d not in writes]
        e = self.eng[en]
        me = e.semidx
        need = {}
        for d in reads:
            if d.w is not None:
                s, v = d.w
                if need.get(s, 0) < v:
                    need[s] = v
        skip = me if en == "pe" else -1
        for d in writes:
            if d.w is not None:
                s, v = d.w
                if s != skip and need.get(s, 0) < v:
                    need[s] = v
            for s, v in d.r.items():
                if s != skip and need.get(s, 0) < v:
                    need[s] = v
        self._wait(e, need)
        inst = fn()
        self.nops += 1
        if signal:
            e.tick += 1
            inst.then_inc(self.sems[me], 1)
            e.pending = False
            t = e.tick
        else:
            e.pending = True
            t = e.tick + 1
        for d in reads:
            if d.r.get(me, 0) < t:
                d.r[me] = t
        for d in writes:
            d.w = (me, t)
            d.r = {}
        return inst

    def dma(self, en, out, in_, sem, reads=(), writes=(), **kw):
        if not self.enabled:
            return None
        e = self.eng[en]
        need = {}
        for d in reads:
            if d.w is not None:
                s, v = d.w
                if need.get(s, 0) < v:
                    need[s] = v
        for d in writes:
            if d.w is not None:
                s, v = d.w
                if need.get(s, 0) < v:
                    need[s] = v
            for s, v in d.r.items():
                if need.get(s, 0) < v:
                    need[s] = v
        self._wait(e, need)
        inst = e.be.dma_start(out=out, in_=in_, **kw)
        self.semcum[sem] += 16
        inst.then_inc(self.sems[sem], 16)
        t = self.semcum[sem]
        for d in reads:
            if d.r.get(sem, 0) < t:
                d.r[sem] = t
        for d in writes:
            d.w = (sem, t)
            d.r = {}
        return inst

    def wait_all(self, en, deps):
        e = self.eng[en]
        need = {}
        for d in deps:
            if d.w is not None:
                s, v = d.w
                if need.get(s, 0) < v:
                    need[s] = v
            for s, v in d.r.items():
                if need.get(s, 0) < v:
                    need[s] = v
        self._wait(e, need)

    def barrier(self, dma_sems=()):
        names = ["pe", "act", "dve", "pool"]
        for e in self.eng.values():
            need = {}
            for n in names:
                o = self.eng[n]
                assert not o.pending
                if o is not e and o.tick > 0:
                    need[o.semidx] = o.tick
            for s in dma_sems:
                if self.semcum[s] > 0:
                    need[s] = self.semcum[s]
            self._wait(e, need)


def tsl(start, step):
    return slice(start, start + 127 * step + 1, step)


def build_program(nseq=SEQ_PER_CORE, depth=DEPTH, do_ffn=True, final_norm=True, out_src="x", nblocks_cast=None):
    nc = bass.Bass("TRN2", target_bir_lowering=False)
    x_d = nc.dram_tensor("x", [nseq, S, D], F32, kind="ExternalInput").ap()
    wbig_d = nc.dram_tensor("wbig", [DEPTH * NBLK * 128, 1024], F32, kind="ExternalInput").ap()
    sp_d = nc.dram_tensor("sp", [128, DEPTH * NSP], F32, kind="ExternalInput").ap()
    fg_d = nc.dram_tensor("fg", [128, 8], F32, kind="ExternalInput").ap()
    lruw_d = nc.dram_tensor("lruw", [128, DEPTH * 6 * 128], F32, kind="ExternalInput").ap()
    ident_d = nc.dram_tensor("ident", [128, 128], F32, kind="ExternalInput").ap()
    bias_d = nc.dram_tensor("abias", [128, 3 * 512], F32, kind="ExternalInput").ap()
    y_d = nc.dram_tensor("y", [nseq, S, D], F32, kind="ExternalOutput").ap()
    wscl = [nc.dram_tensor("wsc%d" % l, [NBLK * 128, 1024], BF16, kind="Internal").ap() for l in range(DEPTH)]

    with ExitStack() as es:
        fw = FW(nc, es)

        _tn = [0]

        def T(stack, name, shape, dt):
            _tn[0] += 1
            return stack.enter_context(nc.sbuf_tensor("%s_%d" % (name, _tn[0]), shape, dt))

        banks = [es.enter_context(nc.psum_tensor("bank%d" % i, [128, 512], F32)) for i in range(8)]
        bdep = [Dep(excl=True) for _ in range(8)]
        bank_rr = [0]

        def nb():
            b = bank_rr[0]
            bank_rr[0] = (b + 1) % 8
            return b

        xT = T(es, "xT", [128, 8, S], F32)
        xdep = [[Dep() for _ in range(4)] for _ in range(8)]
        hT = T(es, "hT", [128, 8, S], BF16)
        hdep = [[Dep() for _ in range(4)] for _ in range(8)]
        ident = T(es, "ident", [128, 128], F32)
        onesf = T(es, "onesf", [128, 128], F32)
        onesb = T(es, "onesb", [128, 128], BF16)
        abias = T(es, "abias", [128, 3, 512], F32)
        spt = T(es, "spt", [128, DEPTH, NSP], F32)
        fgt = T(es, "fgt", [128, 8], F32)
        lruw = T(es, "lruw", [128, DEPTH * 6, 128], BF16)
        cct = T(es, "cct", [128, DEPTH, 3], F32)
        negM = T(es, "negM", [128, 1], F32)
        mx = T(es, "mx", [128, 24], F32)
        qk2 = T(es, "qk2", [128, 4], F32)
        n_acc = T(es, "n_acc", [128, 512], F32)
        n_sq = [T(es, "n_sq%d" % i, [128, 512], F32) for i in range(2)]
        n_rt = T(es, "n_rt", [128, 512], F32)
        cdep = Dep()
        d_negM = Dep()
        d_mx = Dep()
        d_qk2 = Dep()
        d_nacc = Dep()
        d_nsq = [Dep(), Dep()]
        d_nrt = Dep()

        s_const = fw.new_sem("const")

        fw.dma("sp", ident[:], ident_d, s_const, writes=[cdep])
        fw.dma("sp", abias[:].rearrange("p a b -> p (a b)"), bias_d, s_const, writes=[cdep])
        fw.dma("sp", spt[:].rearrange("p a b -> p (a b)"), sp_d, s_const, writes=[cdep])
        fw.dma("sp", fgt[:], fg_d, s_const, writes=[cdep])
        fw.op("pool", lambda: nc.gpsimd.memset(onesf[:], 1.0), writes=[cdep])
        fw.op("pool", lambda: nc.gpsimd.memset(onesb[:], 1.0), writes=[cdep])

        with ExitStack() as ph:
            NS = 6
            stf = [T(ph, "stf%d" % i, [128, 1024], F32) for i in range(NS)]
            stb = [T(ph, "stb%d" % i, [128, 1024], BF16) for i in range(NS)]
            dstf = [Dep() for _ in range(NS)]
            dstb = [Dep() for _ in range(NS)]
            s_ld = [fw.new_sem("cl%d" % i) for i in range(NS)]
            s_st = [fw.new_sem("cs%d" % i) for i in range(NS)]
            lw32 = T(ph, "lw32", [128, DEPTH * 6 * 128], F32)
            dlw = Dep()
            s_lw = fw.new_sem("lw")
            fw.dma("sp", lw32[:], lruw_d, s_lw, writes=[dlw])
            fw.op("dve", lambda: nc.vector.tensor_copy(lruw[:].rearrange("p a b -> p (a b)"), lw32[:]),
                  reads=[dlw], writes=[cdep])
            cc_e = T(ph, "cc_e", [128, DEPTH, 3], F32)
            dcc = Dep()
            for l in range(DEPTH):
                lam = spt[:, l, SP_LAM:SP_LAM + 3]
                fw.op("act", lambda: nc.scalar.activation(cc_e[:, l, :], lam, AF.Exp, scale=-1.0),
                      reads=[cdep], writes=[dcc])
                fw.op("act", lambda: nc.scalar.activation(cc_e[:, l, :], cc_e[:, l, :], AF.Ln, bias=1.0),
                      reads=[dcc], writes=[dcc])
                fw.op("dve", lambda: nc.vector.tensor_scalar(cct[:, l, :], cc_e[:, l, :], -8.0, None, ALU.mult),
                      reads=[dcc], writes=[cdep])
            nblocks = DEPTH * NBLK if nblocks_cast is None else nblocks_cast

            def issue_load(b):
                sl = b % NS
                fw.dma("sp", stf[sl][:], wbig_d[b * 128:(b + 1) * 128, :], s_ld[sl], writes=[dstf[sl]])

            for b in range(min(NS, nblocks)):
                issue_load(b)
            for b in range(nblocks):
                sl = b % NS
                if b % 2 == 0:
                    fw.op("dve", lambda: nc.vector.tensor_copy(stb[sl][:], stf[sl][:]), reads=[dstf[sl]], writes=[dstb[sl]])
                else:
                    fw.op("act", lambda: nc.scalar.copy(stb[sl][:], stf[sl][:]), reads=[dstf[sl]], writes=[dstb[sl]])
                bl = b % NBLK
                fw.dma("sp", wscl[b // NBLK][bl * 128:(bl + 1) * 128, :], stb[sl][:], s_st[sl], reads=[dstb[sl]])
                if b + NS < nblocks:
                    issue_load(b + NS)
            fw.barrier(dma_sems=s_st + [s_const, s_lw])

        def tok(tg):
            return slice(tg * 512, (tg + 1) * 512)

        def rmsnorm_to_hT(gain_ap, emit_out):
            for tg in range(4):
                t = tok(tg)
                for c in range(8):
                    if c == 0:
                        fw.op("act", lambda: nc.scalar.activation(n_acc[:], xT[:, c, t], AF.Square),
                              reads=[xdep[c][tg]], writes=[d_nacc])
                    else:
                        sq, dsq = n_sq[c % 2], d_nsq[c % 2]
                        fw.op("act", lambda: nc.scalar.activation(sq[:], xT[:, c, t], AF.Square),
                              reads=[xdep[c][tg]], writes=[dsq])
                        fw.op("dve", lambda: nc.vector.tensor_tensor(n_acc[:], n_acc[:], sq[:], ALU.add),
                              reads=[d_nacc, dsq], writes=[d_nacc])
                b = nb()
                fw.op("pe", lambda: nc.tensor.matmul(banks[b][:], onesf[:], n_acc[:], start=True, stop=True),
                      reads=[d_nacc, cdep], writes=[bdep[b]])
                fw.op("act", lambda: nc.scalar.activation(n_rt[:], banks[b][:], AF.Sqrt, bias=RMS_EPS, scale=1.0 / D),
                      reads=[bdep[b]], writes=[d_nrt])
                fw.op("dve", lambda: nc.vector.reciprocal(n_rt[:], n_rt[:]), reads=[d_nrt], writes=[d_nrt])
                emit_out(tg)

        def norm_hT(gain):
            def out(tg):
                t = tok(tg)
                for c in range(8):
                    fw.op("dve", lambda: nc.vector.scalar_tensor_tensor(
                        hT[:, c, t], xT[:, c, t], gain[:, c:c + 1], n_rt[:], ALU.mult, ALU.mult),
                        reads=[xdep[c][tg], d_nrt, cdep], writes=[hdep[c][tg]])
            rmsnorm_to_hT(gain, out)

        class WStream:
            def __init__(self, stack, name, nslots, width, items):
                self.tiles = [T(stack, "%s%d" % (name, i), [128, width], BF16) for i in range(nslots)]
                self.deps = [Dep() for _ in range(nslots)]
                self.sems = [sem_pool("%s%d" % (name, i)) for i in range(nslots)]
                self.items = items
                self.nslots = nslots
                self.issued = 0
                self.width = width

            def _issue(self, i):
                sl = i % self.nslots
                row0, rpp = self.items[i]
                src = wscl[row0 // (NBLK * 128)][row0 % (NBLK * 128):row0 % (NBLK * 128) + 128 * rpp, :]
                if rpp > 1:
                    src = src.rearrange("(p r) c -> p (r c)", r=rpp)
                fw.dma("sp", self.tiles[sl][:], src, self.sems[sl], writes=[self.deps[sl]])

            def get(self, i):
                while self.issued < min(len(self.items), i + self.nslots):
                    self._issue(self.issued)
                    self.issued += 1
                sl = i % self.nslots
                return self.tiles[sl], self.deps[sl]

        _sem_pool = {}

        def sem_pool(name):
            if name not in _sem_pool:
                _sem_pool[name] = fw.new_sem(name)
            return _sem_pool[name]

        def evac_alt(i):
            return "act" if i % 2 == 0 else "dve"

        def copy_op(en, out, in_):
            if en == "act":
                return lambda: nc.scalar.copy(out, in_)
            return lambda: nc.vector.tensor_copy(out, in_)

        for s in range(nseq):
            with ExitStack() as ph:
                xst = [T(ph, "xst%d" % i, [128, 4, D], F32) for i in range(2)]
                dxst = [Dep(), Dep()]
                s_x = [sem_pool("xs%d" % i) for i in range(2)]
                ev = 0
                for tg in range(4):
                    sl = tg % 2
                    src = x_d[s, tg * 512:(tg + 1) * 512, :].rearrange("(t p) f -> p t f", p=128)
                    fw.dma("sp", xst[sl][:], src, s_x[sl], writes=[dxst[sl]])
                    for c in range(8):
                        b = nb()
                        for tt in range(4):
                            fw.op("pe", lambda: nc.tensor.transpose(banks[b][:, tt * 128:(tt + 1) * 128],
                                                                     xst[sl][:, tt, c * 128:(c + 1) * 128], ident[:]),
                                  reads=[dxst[sl], cdep], writes=[bdep[b]], signal=(tt == 3))
                        en = evac_alt(ev)
                        ev += 1
                        fw.op(en, copy_op(en, xT[:, c, tok(tg)], banks[b][:]), reads=[bdep[b]], writes=[xdep[c][tg]])
                fw.barrier()

            for l in range(depth):
                rbase = l * NBLK * 128
                gsp = spt[:, l, :]
                with ExitStack() as pa:
                    mixT = T(pa, "mixT", [128, 8, S], BF16)
                    mdep = [[Dep() for _ in range(4)] for _ in range(8)]
                    order = [0, 1, 2, 3, 4, 5, 6, 7, 8, 11, 9, 12, 10, 13, 16, 14, 17, 15, 18] + [BLK_OUT + m for m in range(8)]
                    ws = WStream(pa, "wa", 3, 1024, [(rbase + j * 128, 1) for j in order])
                    wi = [0]

                    def next_w():
                        t_, d_ = ws.get(wi[0])
                        wi[0] += 1
                        return t_[:].rearrange("p (k n) -> p k n", k=8), d_

                    norm_hT(gsp[:, SP_G1:SP_G1 + 8])

                    def zchunk(evac):
                        wt, wd = next_w()
                        bs = [nb() for _ in range(4)]
                        for kc in range(8):
                            for tg in range(4):
                                b = bs[tg]
                                fw.op("pe", lambda: nc.tensor.matmul(banks[b][:], wt[:, kc, :], hT[:, kc, tok(tg)],
                                                                     start=(kc == 0), stop=(kc == 7)),
                                      reads=[wd, hdep[kc][tg]], writes=[bdep[b]], signal=(kc == 7))
                        for tg in range(4):
                            evac(tg, bs[tg])

                    import os as _os
                    _en = _os.environ.get("KPH", "acl")
                    with ExitStack() as pb:
                      if "a" in _en:
                            qk = T(pb, "qk", [128, 6, S], BF16)
                            dqk = [Dep() for _ in range(6)]
                            vtok = T(pb, "vtok", [128, 3, 16, 128], BF16)
                            dv = [[Dep() for _ in range(4)] for _ in range(3)]
                            zb = T(pb, "zb", [128, S], F32)
                            dzb = [Dep() for _ in range(4)]
                            tmp = [T(pb, "atmp%d" % i, [128, 512], F32) for i in range(2)]
                            dtmp = [Dep(), Dep()]
                            pt = [T(pb, "apt%d" % i, [128, 512], BF16) for i in range(2)]
                            dpt = [Dep(), Dep()]
                            sqb = [T(pb, "sqb%d" % i, [128, 512], BF16) for i in range(2)]
                            dsqb = [Dep(), Dep()]
                            cnt = [0]

                            _at = _os.environ.get("KAT", "1234")
                            fw.enabled = "1" in _at
                            for ci in range(6):
                                def ev_qk(tg, b, ci=ci):
                                    t = tok(tg)
                                    fw.op("act", lambda: nc.scalar.copy(qk[:, ci, t], banks[b][:]),
                                          reads=[bdep[b]], writes=[dqk[ci]])
                                    i2 = cnt[0] % 2
                                    cnt[0] += 1
                                    fw.op("pool", lambda: nc.gpsimd.tensor_tensor(sqb[i2][:], qk[:, ci, t], qk[:, ci, t], ALU.mult),
                                          reads=[dqk[ci]], writes=[dsqb[i2]])
                                    b2 = nb()
                                    fw.op("pe", lambda: nc.tensor.matmul(banks[b2][:], onesb[:], sqb[i2][:], start=True, stop=True),
                                          reads=[dsqb[i2], cdep], writes=[bdep[b2]])
                                    col = ci * 4 + tg
                                    fw.op("dve", lambda: nc.vector.tensor_reduce(mx[:, col:col + 1], banks[b2][:], AX.X, ALU.max),
                                          reads=[bdep[b2]], writes=[d_mx])
                                zchunk(ev_qk)
                            fw.op("dve", lambda: nc.vector.tensor_reduce(qk2[:, 0:1], mx[:, 0:12], AX.X, ALU.max),
                                  reads=[d_mx], writes=[d_qk2])
                            fw.op("dve", lambda: nc.vector.tensor_reduce(qk2[:, 1:2], mx[:, 12:24], AX.X, ALU.max),
                                  reads=[d_mx], writes=[d_qk2])
                            fw.op("dve", lambda: nc.vector.tensor_tensor(qk2[:, 2:3], qk2[:, 0:1], qk2[:, 1:2], ALU.mult),
                                  reads=[d_qk2], writes=[d_qk2])
                            fw.op("act", lambda: nc.scalar.activation(qk2[:, 3:4], qk2[:, 2:3], AF.Sqrt),
                                  reads=[d_qk2], writes=[d_qk2])
                            fw.op("dve", lambda: nc.vector.tensor_scalar(negM[:], qk2[:, 3:4], -0.125 * 1.05, None, ALU.mult),
                                  reads=[d_qk2], writes=[d_negM])

                            fw.enabled = "2" in _at
                            for g in range(3):
                                d = DIL[g]
                                nbk = 16 // d
                                wt, wd = next_w()
                                for t4 in range(4):
                                    b = nb()
                                    for tj in range(4):
                                        ti = t4 * 4 + tj
                                        r, n = ti // nbk, ti % nbk
                                        st0 = r + d * 128 * n
                                        if g == 0:
                                            tgs = [n // 4]
                                        elif g == 1:
                                            tgs = [n]
                                        else:
                                            tgs = [0, 1, 2, 3]
                                        for kc in range(8):
                                            fw.op("pe", lambda: nc.tensor.matmul(banks[b][:, tj * 128:(tj + 1) * 128],
                                                                                 hT[:, kc, tsl(st0, d)], wt[:, kc, :],
                                                                                 start=(kc == 0), stop=(kc == 7)),
                                                  reads=[wd] + [hdep[kc][x] for x in tgs], writes=[bdep[b]],
                                                  signal=(kc == 7 and tj == 3))
                                    fw.op("dve", lambda: nc.vector.tensor_copy(
                                        vtok[:, g, t4 * 4:(t4 + 1) * 4, :].rearrange("p a b -> p (a b)"), banks[b][:]),
                                        reads=[bdep[b]], writes=[dv[g][t4]])

                            fw.enabled = "3" in _at
                            tiles = [(g, ti) for g in range(3) for ti in range(16)]
                            stt = {}

                            def stage1(idx):
                                g, ti = tiles[idx]
                                d = DIL[g]
                                nbk = 16 // d
                                r, n = ti // nbk, ti % nbk
                                qs = r + d * 128 * n
                                if g == 0:
                                    tgs = [n // 4]
                                elif g == 1:
                                    tgs = [n]
                                else:
                                    tgs = [0, 1, 2, 3]
                                kbs = [1] if n == 0 else [0, 1]
                                bsth = [nb(), nb()]
                                i2 = idx % 2
                                cols = slice(0, 512) if n > 0 else slice(256, 512)
                                stt[idx] = (g, ti, d, n, qs, tgs, kbs, i2)
                                for h in range(2):
                                    bst = bsth[h]
                                    for ik, kb in enumerate(kbs):
                                        kti = ti - 1 if kb == 0 else ti
                                        ks = (kti // nbk) + d * 128 * (kti % nbk)
                                        fw.op("pe", lambda: nc.tensor.matmul(
                                            banks[bst][:, kb * 128:(kb + 1) * 128],
                                            qk[64 * h:64 * h + 64, 3 + g, tsl(ks, d)],
                                            qk[64 * h:64 * h + 64, g, tsl(qs, d)], start=True, stop=True),
                                            reads=[dqk[g], dqk[3 + g]], writes=[bdep[bst]], signal=(ik == len(kbs) - 1))
                                for h in range(2):
                                    bst = bsth[h]
                                    k0 = kbs[0]
                                    nk = len(kbs)
                                    tview = tmp[i2][:].rearrange("p (a b c) -> p a b c", a=2, b=2)[:, k0:2, h, :]
                                    bview = abias[:, g, :].rearrange("p (a b c) -> p a b c", a=2, b=2)[:, k0:2, h, :]
                                    sview = banks[bst][:, k0 * 128:256].rearrange("p (a c) -> p a c", a=nk)
                                    fw.op("dve", lambda: nc.vector.scalar_tensor_tensor(
                                        tview, sview, 0.125, bview, ALU.mult, ALU.add),
                                        reads=[bdep[bst], cdep], writes=[dtmp[i2]])
                                fw.op("act", lambda: nc.scalar.activation(pt[i2][:, cols], tmp[i2][:, cols], AF.Exp, bias=negM[:, 0:1]),
                                      reads=[dtmp[i2], d_negM], writes=[dpt[i2]])

                            def stage2(idx):
                                g, ti, d, n, qs, tgs, kbs, i2 = stt.pop(idx)
                                bu = nb()
                                vdeps = [dv[g][ti // 4]] + ([dv[g][(ti - 1) // 4]] if n > 0 else [])
                                for part in range(2):
                                    for h in range(2):
                                        for ix, kb in enumerate(kbs):
                                            kti = ti - 1 if kb == 0 else ti
                                            o0 = (kb * 2 + h) * 128
                                            lhs = vtok[:, g, kti, 64 * h:64 * h + 64] if part == 0 else onesb[:, 0:64]
                                            last = (part == 1 and h == 1 and ix == len(kbs) - 1)
                                            fw.op("pe", lambda: nc.tensor.matmul(
                                                banks[bu][64 * h:64 * h + 64, part * 128:(part + 1) * 128],
                                                lhs, pt[i2][:, o0:o0 + 128],
                                                start=(ix == 0), stop=(ix == len(kbs) - 1)),
                                                reads=[dpt[i2], cdep] + vdeps, writes=[bdep[bu]], signal=last)
                                fw.op("act", lambda: nc.scalar.copy(mixT[:, g, tsl(qs, d)], banks[bu][:, 0:128]),
                                      reads=[bdep[bu]], writes=[mdep[g][x] for x in tgs])
                                if g == 0:
                                    fw.op("dve", lambda: nc.vector.tensor_copy(zb[:, tsl(qs, d)], banks[bu][:, 128:256]),
                                          reads=[bdep[bu]], writes=[dzb[x] for x in tgs])
                                else:
                                    fw.op("dve", lambda: nc.vector.tensor_tensor(
                                        zb[:, tsl(qs, d)], banks[bu][:, 128:256], zb[:, tsl(qs, d)], ALU.add),
                                        reads=[bdep[bu]] + [dzb[x] for x in tgs], writes=[dzb[x] for x in tgs])

                            for idx in range(len(tiles) + 1):
                                if idx < len(tiles):
                                    stage1(idx)
                                if idx >= 1:
                                    stage2(idx - 1)
                            fw.enabled = "4" in _at
                            for tg in range(4):
                                t = tok(tg)
                                fw.op("dve", lambda: nc.vector.reciprocal(zb[:, t], zb[:, t]), reads=[dzb[tg]], writes=[dzb[tg]])
                                for g in range(3):
                                    fw.op("dve", lambda: nc.vector.tensor_tensor(mixT[:, g, t], mixT[:, g, t], zb[:, t], ALU.mult),
                                          reads=[mdep[g][tg], dzb[tg]], writes=[mdep[g][tg]])
                            fw.barrier()

                    fw.enabled = True
                    with ExitStack() as pb:
                      if "c" in _en:
                            PADC = 30
                            u = T(pb, "cu", [128, 2, PADC + S], F32)
                            du = [[Dep() for _ in range(4)] for _ in range(2)]
                            acc = T(pb, "cacc", [128, 2, S], F32)
                            dacc = [[Dep() for _ in range(2)] for _ in range(2)]
                            sg = T(pb, "csg", [128, S], F32)
                            dsg = [Dep() for _ in range(4)]
                            csq = [T(pb, "csq%d" % i, [128, 512], F32) for i in range(2)]
                            dcsq = [Dep(), Dep()]
                            cmean = T(pb, "cmean", [128, 512], F32)
                            cmsq = T(pb, "cmsq", [128, 512], F32)
                            cvar = T(pb, "cvar", [128, 512], F32)
                            cta = T(pb, "cta", [128, 512], F32)
                            ctb = T(pb, "ctb", [128, 512], F32)
                            dmean, dmsq, dvar, dta, dtb = Dep(), Dep(), Dep(), Dep(), Dep()
                            dpad = Dep()
                            fw.op("pool", lambda: nc.gpsimd.memset(u[:, :, 0:PADC], 0.0), writes=[dpad])
                            for c in range(2):
                                def ev_g(tg, b):
                                    fw.op("act", lambda: nc.scalar.activation(sg[:, tok(tg)], banks[b][:], AF.Sigmoid),
                                          reads=[bdep[b]], writes=[dsg[tg]])
                                zchunk(ev_g)

                                def ev_a(tg, b, c=c):
                                    fw.op("dve", lambda: nc.vector.tensor_tensor(
                                        u[:, c, PADC + tg * 512:PADC + (tg + 1) * 512], banks[b][:], sg[:, tok(tg)], ALU.mult),
                                        reads=[bdep[b], dsg[tg]], writes=[du[c][tg]])
                                zchunk(ev_a)
                            for c in range(2):
                                for k in range(31):
                                    for hh in range(2):
                                        o = acc[:, c, hh * 1024:(hh + 1) * 1024]
                                        i0 = u[:, c, k + hh * 1024:k + hh * 1024 + 1024]
                                        rd = [du[c][x] for x in range(4)] + [dpad, cdep]
                                        wcol = gsp[:, SP_CW + c * 31 + k:SP_CW + c * 31 + k + 1]
                                        if k == 0:
                                            fw.op("dve", lambda: nc.vector.tensor_scalar(
                                                o, i0, wcol, gsp[:, SP_CB + c:SP_CB + c + 1], ALU.mult, ALU.add),
                                                reads=rd, writes=[dacc[c][hh]])
                                        else:
                                            fw.op("dve", lambda: nc.vector.scalar_tensor_tensor(
                                                o, i0, wcol, o, ALU.mult, ALU.add),
                                                reads=rd + [dacc[c][hh]], writes=[dacc[c][hh]])
                            for tg in range(4):
                                t = tok(tg)
                                hh = tg // 2
                                b1 = nb()
                                for c in range(2):
                                    fw.op("pe", lambda: nc.tensor.matmul(banks[b1][:], onesf[:], acc[:, c, t], start=(c == 0), stop=(c == 1)),
                                          reads=[dacc[c][hh], cdep], writes=[bdep[b1]], signal=(c == 1))
                                for c in range(2):
                                    fw.op("act", lambda: nc.scalar.activation(csq[c][:], acc[:, c, t], AF.Square),
                                          reads=[dacc[c][hh]], writes=[dcsq[c]])
                                b2 = nb()
                                for c in range(2):
                                    fw.op("pe", lambda: nc.tensor.matmul(banks[b2][:], onesf[:], csq[c][:], start=(c == 0), stop=(c == 1)),
                                          reads=[dcsq[c], cdep], writes=[bdep[b2]], signal=(c == 1))
                                fw.op("dve", lambda: nc.vector.tensor_scalar(cmean[:], banks[b1][:], 1.0 / 256, None, ALU.mult),
                                      reads=[bdep[b1]], writes=[dmean])
                                fw.op("dve", lambda: nc.vector.tensor_tensor(cmsq[:], cmean[:], cmean[:], ALU.mult),
                                      reads=[dmean], writes=[dmsq])
                                fw.op("dve", lambda: nc.vector.scalar_tensor_tensor(cvar[:], banks[b2][:], 1.0 / 256, cmsq[:], ALU.mult, ALU.subtract),
                                      reads=[bdep[b2], dmsq], writes=[dvar])
                                fw.op("act", lambda: nc.scalar.activation(cvar[:], cvar[:], AF.Sqrt, bias=LN_EPS),
                                      reads=[dvar], writes=[dvar])
                                fw.op("dve", lambda: nc.vector.reciprocal(cvar[:], cvar[:]), reads=[dvar], writes=[dvar])
                                for c in range(2):
                                    fw.op("dve", lambda: nc.vector.tensor_tensor(cta[:], acc[:, c, t], cmean[:], ALU.subtract),
                                          reads=[dacc[c][hh], dmean], writes=[dta])
                                    fw.op("dve", lambda: nc.vector.tensor_tensor(ctb[:], cta[:], cvar[:], ALU.mult),
                                          reads=[dta, dvar], writes=[dtb])
                                    fw.op("act", lambda: nc.scalar.activation(
                                        mixT[:, 3 + c, t], ctb[:], AF.Silu,
                                        bias=gsp[:, SP_LNB + c:SP_LNB + c + 1], scale=gsp[:, SP_LNG + c:SP_LNG + c + 1]),
                                        reads=[dtb, cdep], writes=[mdep[3 + c][tg]])
                            fw.barrier()

                    with ExitStack() as pb:
                      if "l" in _en:
                            PADL = 3
                            gg = T(pb, "lgg", [128, S], F32)
                            xr = T(pb, "lxr", [128, PADL + S], F32)
                            ul = T(pb, "lul", [128, S], F32)
                            ub = T(pb, "lub", [128, S], BF16)
                            r_ = T(pb, "lr", [128, S], F32)
                            i_ = T(pb, "li", [128, S], F32)
                            t1 = T(pb, "lt1", [128, S], F32)
                            dgg = [Dep() for _ in range(4)]
                            dxr = [Dep() for _ in range(4)]
                            dpad = Dep()
                            dul = [Dep(), Dep()]
                            dub = [Dep(), Dep()]
                            dr = [Dep() for _ in range(4)]
                            di = [Dep() for _ in range(4)]
                            dt1 = [Dep() for _ in range(4)]
                            fw.op("pool", lambda: nc.gpsimd.memset(xr[:, 0:PADL], 0.0), writes=[dpad])
                            for c in range(3):
                                def ev_gate(tg, b):
                                    fw.op("act", lambda: nc.scalar.activation(gg[:, tok(tg)], banks[b][:], AF.Gelu_apprx_tanh),
                                          reads=[bdep[b]], writes=[dgg[tg]])
                                zchunk(ev_gate)

                                def ev_x(tg, b):
                                    fw.op("dve", lambda: nc.vector.tensor_copy(xr[:, PADL + tg * 512:PADL + (tg + 1) * 512], banks[b][:]),
                                          reads=[bdep[b]], writes=[dxr[tg]])
                                zchunk(ev_x)
                                for k in range(4):
                                    for hh in range(2):
                                        o = ul[:, hh * 1024:(hh + 1) * 1024]
                                        i0 = xr[:, k + hh * 1024:k + hh * 1024 + 1024]
                                        rd = dxr + [dpad, cdep]
                                        wcol = gsp[:, SP_LCW + c * 4 + k:SP_LCW + c * 4 + k + 1]
                                        if k == 0:
                                            fw.op("dve", lambda: nc.vector.tensor_scalar(
                                                o, i0, wcol, gsp[:, SP_LCB + c:SP_LCB + c + 1], ALU.mult, ALU.add),
                                                reads=rd, writes=[dul[hh]])
                                        else:
                                            fw.op("dve", lambda: nc.vector.scalar_tensor_tensor(o, i0, wcol, o, ALU.mult, ALU.add),
                                                  reads=rd + [dul[hh]], writes=[dul[hh]])
                                for hh in range(2):
                                    fw.op("act", lambda: nc.scalar.copy(ub[:, hh * 1024:(hh + 1) * 1024], ul[:, hh * 1024:(hh + 1) * 1024]),
                                          reads=[dul[hh]], writes=[dub[hh]])
                                for tg in range(4):
                                    t = tok(tg)
                                    hh = tg // 2
                                    ba_, bx_ = nb(), nb()
                                    fw.op("pe", lambda: nc.tensor.matmul(banks[ba_][:], lruw[:, l * 6 + c * 2 + 0, :], ub[:, t], start=True, stop=True),
                                          reads=[dub[hh], cdep], writes=[bdep[ba_]])
                                    fw.op("pe", lambda: nc.tensor.matmul(banks[bx_][:], lruw[:, l * 6 + c * 2 + 1, :], ub[:, t], start=True, stop=True),
                                          reads=[dub[hh], cdep], writes=[bdep[bx_]])
                                    fw.op("act", lambda: nc.scalar.activation(r_[:, t], banks[ba_][:], AF.Sigmoid, bias=gsp[:, SP_LBA + c:SP_LBA + c + 1]),
                                          reads=[bdep[ba_], cdep], writes=[dr[tg]])
                                    fw.op("act", lambda: nc.scalar.activation(i_[:, t], banks[bx_][:], AF.Sigmoid, bias=gsp[:, SP_LBX + c:SP_LBX + c + 1]),
                                          reads=[bdep[bx_], cdep], writes=[di[tg]])
                                for tg in range(4):
                                    t = tok(tg)
                                    hh = tg // 2
                                    fw.op("act", lambda: nc.scalar.activation(r_[:, t], r_[:, t], AF.Exp, scale=cct[:, l, c:c + 1]),
                                          reads=[dr[tg], cdep], writes=[dr[tg]])
                                    fw.op("dve", lambda: nc.vector.tensor_tensor(t1[:, t], r_[:, t], r_[:, t], ALU.mult),
                                          reads=[dr[tg]], writes=[dt1[tg]])
                                    fw.op("act", lambda: nc.scalar.activation(t1[:, t], t1[:, t], AF.Sqrt, bias=1.0, scale=-1.0),
                                          reads=[dt1[tg]], writes=[dt1[tg]])
                                    fw.op("dve", lambda: nc.vector.tensor_tensor(i_[:, t], i_[:, t], ul[:, t], ALU.mult),
                                          reads=[di[tg], dul[hh]], writes=[di[tg]])
                                    fw.op("dve", lambda: nc.vector.tensor_tensor(i_[:, t], i_[:, t], t1[:, t], ALU.mult),
                                          reads=[di[tg], dt1[tg]], writes=[di[tg]])
                                fw.op("dve", lambda: nc.vector.tensor_tensor_scan(t1[:], r_[:], i_[:], 0.0, ALU.mult, ALU.add),
                                      reads=dr + di + dt1, writes=dt1)
                                for tg in range(4):
                                    t = tok(tg)
                                    fw.op("dve", lambda: nc.vector.tensor_tensor(mixT[:, 5 + c, t], gg[:, t], t1[:, t], ALU.mult),
                                          reads=[dgg[tg], dt1[tg]], writes=[mdep[5 + c][tg]])
                            fw.barrier()

                    if out_src == "mix" and l == depth - 1:
                        for c in range(8):
                            for tg in range(4):
                                fw.op("dve", lambda: nc.vector.tensor_copy(xT[:, c, tok(tg)], mixT[:, c, tok(tg)]),
                                      reads=[mdep[c][tg]], writes=[xdep[c][tg]])
                    else:
                        for m in range(8):
                            wt, wd = next_w()
                            bs = [nb() for _ in range(4)]
                            for kc in range(8):
                                for tg in range(4):
                                    b = bs[tg]
                                    fw.op("pe", lambda: nc.tensor.matmul(banks[b][:], wt[:, kc, :], mixT[:, kc, tok(tg)],
                                                                         start=(kc == 0), stop=(kc == 7)),
                                          reads=[wd, mdep[kc][tg]], writes=[bdep[b]], signal=(kc == 7))
                            for tg in range(4):
                                b = bs[tg]
                                fw.op("dve", lambda: nc.vector.tensor_tensor(xT[:, m, tok(tg)], banks[b][:], xT[:, m, tok(tg)], ALU.add),
                                      reads=[bdep[b], xdep[m][tg]], writes=[xdep[m][tg]])
                    fw.barrier(dma_sems=ws.sems)

                if do_ffn and not (out_src == "mix" and l == depth - 1):
                    with ExitStack() as pa:
                        aT = T(pa, "aT", [128, 32, 1024], BF16)
                        da = [[Dep() for _ in range(2)] for _ in range(32)]
                        rl = [T(pa, "rl%d" % i, [128, 512], F32) for i in range(2)]
                        drl = [Dep() for _ in range(2)]
                        wsu = WStream(pa, "wu", 3, 1024, [(rbase + (BLK_UP + f) * 128, 1) for f in range(32)] * 2)
                        wsd = WStream(pa, "wd", 2, 4096, [(rbase + (BLK_DOWN + 4 * m) * 128, 4) for m in range(8)] * 2)
                        norm_hT(gsp[:, SP_G2:SP_G2 + 8])
                        iu = 0
                        idn = 0
                        irl = 0
                        for t2 in range(2):
                            for f in range(32):
                                wt_, wd = wsu.get(iu)
                                iu += 1
                                wt = wt_[:].rearrange("p (k n) -> p k n", k=8)
                                bs = [nb(), nb()]
                                for kc in range(8):
                                    for hf in range(2):
                                        tg = t2 * 2 + hf
                                        b = bs[hf]
                                        fw.op("pe", lambda: nc.tensor.matmul(banks[b][:], wt[:, kc, :], hT[:, kc, tok(tg)],
                                                                             start=(kc == 0), stop=(kc == 7)),
                                              reads=[wd, hdep[kc][tg]], writes=[bdep[b]], signal=(kc == 7))
                                for hf in range(2):
                                    b = bs[hf]
                                    ir = irl % 2
                                    irl += 1
                                    fw.op("act", lambda: nc.scalar.activation(rl[ir][:], banks[b][:], AF.Relu),
                                          reads=[bdep[b]], writes=[drl[ir]])
                                    fw.op("pool", lambda: nc.gpsimd.tensor_tensor(aT[:, f, hf * 512:(hf + 1) * 512], rl[ir][:], rl[ir][:], ALU.mult),
                                          reads=[drl[ir]], writes=[da[f][hf]])
                            for m in range(8):
                                wt_, wd = wsd.get(idn)
                                idn += 1
                                wt = wt_[:].rearrange("p (k n) -> p k n", k=32)
                                bs = [nb(), nb()]
                                for f in range(32):
                                    for hf in range(2):
                                        b = bs[hf]
                                        fw.op("pe", lambda: nc.tensor.matmul(banks[b][:], wt[:, f, :], aT[:, f, hf * 512:(hf + 1) * 512],
                                                                             start=(f == 0), stop=(f == 31)),
                                              reads=[wd, da[f][hf]], writes=[bdep[b]], signal=(f == 31))
                                for hf in range(2):
                                    tg = t2 * 2 + hf
                                    b = bs[hf]
                                    fw.op("dve", lambda: nc.vector.tensor_tensor(xT[:, m, tok(tg)], banks[b][:], xT[:, m, tok(tg)], ALU.add),
                                          reads=[bdep[b], xdep[m][tg]], writes=[xdep[m][tg]])
                        fw.barrier(dma_sems=wsu.sems + wsd.sems)

            with ExitStack() as ph:
                yst = [T(ph, "yst%d" % i, [128, 4, D], F32) for i in range(2)]
                dyst = [Dep(), Dep()]
                s_y = [sem_pool("ys%d" % i) for i in range(2)]
                ytmp = [T(ph, "ytmp%d" % i, [128, 512], F32) for i in range(2)]
                dytmp = [Dep(), Dep()]
                ev = [0]

                def emit_store(tg):
                    t = tok(tg)
                    sl = tg % 2
                    for c in range(8):
                        i2 = c % 2
                        if final_norm:
                            fw.op("dve", lambda: nc.vector.scalar_tensor_tensor(
                                ytmp[i2][:], xT[:, c, t], fgt[:, c:c + 1], n_rt[:], ALU.mult, ALU.mult),
                                reads=[xdep[c][tg], d_nrt, cdep], writes=[dytmp[i2]])
                            src, sdep = ytmp[i2], dytmp[i2]
                        b = nb()
                        for tt in range(4):
                            if final_norm:
                                in_ = src[:, tt * 128:(tt + 1) * 128]
                                rd = [sdep, cdep]
                            else:
                                in_ = xT[:, c, tg * 512 + tt * 128:tg * 512 + (tt + 1) * 128]
                                rd = [xdep[c][tg], cdep]
                            fw.op("pe", lambda: nc.tensor.transpose(banks[b][:, tt * 128:(tt + 1) * 128], in_, ident[:]),
                                  reads=rd, writes=[bdep[b]], signal=(tt == 3))
                        en = evac_alt(ev[0])
                        ev[0] += 1
                        fw.op(en, copy_op(en, yst[sl][:, :, c * 128:(c + 1) * 128],
                                          banks[b][:].rearrange("p (a b) -> p a b", a=4)),
                              reads=[bdep[b]], writes=[dyst[sl]])
                    dst = y_d[s, tg * 512:(tg + 1) * 512, :].rearrange("(t p) f -> p t f", p=128)
                    fw.dma("sp", dst, yst[sl][:], s_y[sl], reads=[dyst[sl]])

                if final_norm:
                    rmsnorm_to_hT(None, emit_store)
                else:
                    for tg in range(4):
                        emit_store(tg)
                fw.barrier(dma_sems=s_y)
        print("program: ops=%d waits=%d sems=%d" % (fw.nops, fw.nwaits, len(fw.sems)))
    return nc


def prep_shared(inp):
    f32 = np.float32
    per_layer = []
    for l in range(DEPTH):
        w_in = np.asarray(inp["w_in"][l], f32).reshape(8, 128, 19, 128).transpose(2, 1, 0, 3)
        w_out = np.asarray(inp["w_out"][l], f32).reshape(8, 128, 8, 128).transpose(2, 1, 0, 3)
        w_up = np.asarray(inp["w_up"][l], f32).reshape(8, 128, 32, 128).transpose(2, 1, 0, 3)
        w_down = np.asarray(inp["w_down"][l], f32).reshape(32, 128, 8, 128).transpose(2, 1, 0, 3)
        per_layer.append(np.concatenate([w_in.ravel(), w_out.ravel(), w_up.ravel(), w_down.ravel()]))
    wbig = np.ascontiguousarray(np.stack(per_layer).reshape(DEPTH * NBLK * 128, 1024))

    def fm(v, nch):
        return np.asarray(v, f32).reshape(nch, 128).T

    sp = np.zeros((128, DEPTH, NSP), f32)
    lruw = np.zeros((128, DEPTH, 3, 2, 128), f32)
    for l in range(DEPTH):
        sp[:, l, SP_G1:SP_G1 + 8] = fm(inp["norm1_g"][l], 8)
        sp[:, l, SP_G2:SP_G2 + 8] = fm(inp["norm2_g"][l], 8)
        cw = np.asarray(inp["conv_dw_w"][l], f32)
        sp[:, l, SP_CW:SP_CW + 62] = cw.reshape(31, 2, 128).transpose(2, 1, 0).reshape(128, 62)
        sp[:, l, SP_CB:SP_CB + 2] = fm(inp["conv_dw_b"][l], 2)
        sp[:, l, SP_LNG:SP_LNG + 2] = fm(inp["conv_ln_g"][l], 2)
        sp[:, l, SP_LNB:SP_LNB + 2] = fm(inp["conv_ln_b"][l], 2)
        lcw = np.asarray(inp["lru_conv_w"][l], f32)
        sp[:, l, SP_LCW:SP_LCW + 12] = lcw.reshape(4, 3, 128).transpose(2, 1, 0).reshape(128, 12)
        sp[:, l, SP_LCB:SP_LCB + 3] = fm(inp["lru_conv_b"][l], 3)
        sp[:, l, SP_LBA:SP_LBA + 3] = fm(inp["lru_ba"][l], 3)
        sp[:, l, SP_LBX:SP_LBX + 3] = fm(inp["lru_bx"][l], 3)
        sp[:, l, SP_LAM:SP_LAM + 3] = fm(inp["lru_lambda"][l], 3)
        for c in range(3):
            for hh in range(2):
                lruw[64 * hh:64 * hh + 64, l, c, 0, 64 * hh:64 * hh + 64] = inp["lru_wa"][l][2 * c + hh]
                lruw[64 * hh:64 * hh + 64, l, c, 1, 64 * hh:64 * hh + 64] = inp["lru_wx"][l][2 * c + hh]
    fg = fm(inp["final_g"], 8)
    ident = np.eye(128, dtype=f32)
    slopes = (2.0 ** (-8.0 * np.arange(1, 7) / 6)).astype(f32)
    kk = np.arange(128)[:, None]
    qq = np.arange(128)[None, :]
    ab = np.zeros((128, 3, 2, 2, 128), f32)
    for g in range(3):
        for h in range(2):
            sl = slopes[2 * g + h]
            dist_prev = qq + 128 - kk
            dist_cur = qq - kk
            bp = -(sl * (DIL[g] * dist_prev).astype(f32))
            bc = -(sl * (DIL[g] * dist_cur).astype(f32))
            ab[:, g, 0, h, :] = np.where(dist_prev <= 128, bp, MASK_NEG)
            ab[:, g, 1, h, :] = np.where(dist_cur >= 0, bc, MASK_NEG)
    return {
        "wbig": wbig,
        "sp": np.ascontiguousarray(sp.reshape(128, DEPTH * NSP)),
        "fg": np.ascontiguousarray(fg),
        "lruw": np.ascontiguousarray(lruw.reshape(128, DEPTH * 6 * 128)),
        "ident": ident,
        "abias": np.ascontiguousarray(ab.reshape(128, 3 * 512)),
    }


_PROG = {}


def kernel(**inputs):
    x = np.ascontiguousarray(np.asarray(inputs["x"], np.float32))
    shared = prep_shared(inputs)
    if "nc" not in _PROG:
        _PROG["nc"] = build_program()
    nc = _PROG["nc"]
    in_maps = []
    for c in range(NCORES):
        m = dict(shared)
        m["x"] = np.ascontiguousarray(x[c * SEQ_PER_CORE:(c + 1) * SEQ_PER_CORE])
        in_maps.append(m)
    res = run_bass_kernel_spmd(nc, in_maps, core_ids=list(range(NCORES)))
    return np.concatenate([np.asarray(r["y"], np.float32) for r in res.results], axis=0)
```
